# Optimizing a Trainium2 kernel written in Bass

```python
import jax, jax.numpy as jnp
from jax import lax
import numpy as np

D_MODEL = 1024
BATCH = 32
SEQ = 2048
DEPTH = 2
DEC_BATCH = 4
DEC_SEQ = 4096
PAST_LEN = 128

GRID_W = 64
ATTN_HEADS = 8
ATTN_HEAD_DIM = 64
ATTN_WIDTH = ATTN_HEADS * ATTN_HEAD_DIM
WIN_ROWS = 8
WIN_COLS = 16
QBLK_COLS = 16
KBLK_COLS = QBLK_COLS + WIN_COLS
N_CBLK = GRID_W // QBLK_COLS
SSM_HEADS = 8
SSM_HEAD_DIM = 64
SSM_WIDTH = SSM_HEADS * SSM_HEAD_DIM
SSM_GROUPS = 2
SSM_STATE = 128
CONV_W = 5
CONV_DIM = SSM_WIDTH + 2 * SSM_GROUPS * SSM_STATE
CHUNK = 128
D_MIX = ATTN_WIDTH + SSM_WIDTH
D_IN_PROJ = 3 * ATTN_WIDTH + SSM_WIDTH + CONV_DIM + 2 * SSM_HEADS
SPLITS = [ATTN_WIDTH, 2 * ATTN_WIDTH, 3 * ATTN_WIDTH, 3 * ATTN_WIDTH + SSM_WIDTH,
          3 * ATTN_WIDTH + SSM_WIDTH + CONV_DIM]
D_FF = 2816
ALPHA = (2 * DEPTH) ** 0.25
BETA = (8 * DEPTH) ** -0.25
LN_EPS = 1e-5
RMS_EPS = 1e-6

kernel_name = "hymba_natten_ssd_macaron_deepnorm_encoder"


def layer_norm(x, g, b):
    x32 = x.astype(jnp.float32)
    mu = jnp.mean(x32, axis=-1, keepdims=True)
    xc = x32 - mu
    var = jnp.mean(xc * xc, axis=-1, keepdims=True)
    y = xc * lax.rsqrt(var + LN_EPS) * g.astype(jnp.float32) + b.astype(jnp.float32)
    return y.astype(x.dtype)


def rms_norm(x, g):
    x32 = x.astype(jnp.float32)
    y = x32 * lax.rsqrt(jnp.mean(x32 * x32, axis=-1, keepdims=True) + RMS_EPS)
    return y * g.astype(jnp.float32)


def swiglu_ffn(x, w_gate, w_up, w_down):
    return (jax.nn.silu(x @ w_gate) * (x @ w_up)) @ w_down


def neighbourhood_attention(q, k, v, rpb):
    bsz, t, h, dh = q.shape
    rows = t // GRID_W
    kr = min(WIN_ROWS, rows)
    qcol = np.arange(GRID_W).reshape(N_CBLK, QBLK_COLS)
    kstart = np.clip(np.arange(N_CBLK) * QBLK_COLS - WIN_COLS // 2, 0, GRID_W - KBLK_COLS)
    kcol = kstart[:, None] + np.arange(KBLK_COLS)
    cstart = np.clip(qcol - WIN_COLS // 2, 0, GRID_W - WIN_COLS)
    col_valid = (kcol[:, None, :] >= cstart[:, :, None]) & (kcol[:, None, :] < cstart[:, :, None] + WIN_COLS)
    col_off = np.clip(kcol[:, None, :] - qcol[:, :, None] + WIN_COLS - 1, 0, 2 * WIN_COLS - 2)
    rpb_cols = jnp.where(col_valid, rpb[:, :, col_off].astype(jnp.float32), -jnp.inf)

    qg = q.reshape(bsz, rows, N_CBLK, QBLK_COLS, h, dh)
    kg_cols = k.reshape(bsz, rows, GRID_W, h, dh)[:, :, kcol]
    vg_cols = v.reshape(bsz, rows, GRID_W, h, dh)[:, :, kcol]

    def row_block(r):
        rs = jnp.clip(r - kr // 2, 0, rows - kr)
        k_blk = lax.dynamic_slice_in_dim(kg_cols, rs, kr, axis=1)
        v_blk = lax.dynamic_slice_in_dim(vg_cols, rs, kr, axis=1)
        q_blk = lax.dynamic_index_in_dim(qg, r, axis=1, keepdims=False)
        row_off = rs + jnp.arange(kr) - r + (WIN_ROWS - 1)
        bias = jnp.take(rpb_cols, row_off, axis=1).transpose(0, 2, 3, 1, 4)
        s = jnp.einsum('bjqhd,brjkhd->bhjqrk', q_blk, k_blk).astype(jnp.float32) + bias[None]
        p = jax.nn.softmax(s.reshape(s.shape[:4] + (kr * KBLK_COLS,)), axis=-1).reshape(s.shape)
        o = jnp.einsum('bhjqrk,brjkhd->bjqhd', p.astype(v.dtype), v_blk)
        return o.reshape(bsz, GRID_W, h, dh)

    out = lax.map(row_block, jnp.arange(rows))
    return jnp.moveaxis(out, 0, 1).reshape(bsz, t, h * dh)


def ssd_chunked(x, dt, a, b_mat, c_mat):
    bsz, t, h, p = x.shape
    g, n = b_mat.shape[2], b_mat.shape[3]
    e = h // g
    nc = t // CHUNK
    xd = (x.astype(jnp.float32) * dt[..., None]).reshape(bsz, nc, CHUNK, g, e, p)
    la = (dt * a).reshape(bsz, nc, CHUNK, g, e).transpose(0, 3, 4, 1, 2)
    a_cum = jnp.cumsum(la, axis=-1)
    bc = b_mat.astype(jnp.float32).reshape(bsz, nc, CHUNK, g, n)
    cc = c_mat.astype(jnp.float32).reshape(bsz, nc, CHUNK, g, n)
    cb = jnp.einsum('bclgn,bcsgn->bgcls', cc, bc)
    seg = a_cum[..., :, None] - a_cum[..., None, :]
    causal = np.tril(np.ones((CHUNK, CHUNK), dtype=bool))
    decay = jnp.exp(jnp.where(causal, seg, -jnp.inf))
    y_diag = jnp.einsum('bgcls,bgecls,bcsgep->bclgep', cb, decay, xd)
    decay_states = jnp.exp(a_cum[..., -1:] - a_cum)
    states = jnp.einsum('bclgn,bgecl,bclgep->bcgepn', bc, decay_states, xd)
    chunk_decay = jnp.exp(a_cum[..., -1])

    def step(hstate, inp):
        st, dec = inp
        return hstate * dec[..., None, None] + st, hstate

    _, prev = lax.scan(step, jnp.zeros_like(states[:, 0]),
                       (jnp.moveaxis(states, 1, 0), jnp.moveaxis(chunk_decay, 3, 0)))
    prev = jnp.moveaxis(prev, 0, 1)
    y_off = jnp.einsum('bclgn,bcgepn,bgecl->bclgep', cc, prev, jnp.exp(a_cum))
    return (y_diag + y_off).reshape(bsz, t, h, p)


def dwconv_centred(u, w, b):
    c = u.shape[-1]
    out = lax.conv_general_dilated(u, w[:, None, :].astype(u.dtype), window_strides=(1,),
                                   padding=[(CONV_W // 2, CONV_W // 2)],
                                   dimension_numbers=('NWC', 'WIO', 'NWC'), feature_group_count=c)
    return out + b


def ssd_mixer(z, xbc, dt_raw, conv_w, conv_b, dt_bias, a_log, d_skip, norm_g):
    bsz, t = z.shape[:2]
    xbc = jax.nn.silu(dwconv_centred(xbc, conv_w, conv_b))
    xs = xbc[..., :SSM_WIDTH].reshape(bsz, t, SSM_HEADS, SSM_HEAD_DIM)
    bm = xbc[..., SSM_WIDTH:SSM_WIDTH + SSM_GROUPS * SSM_STATE].reshape(bsz, t, SSM_GROUPS, SSM_STATE)
    cm = xbc[..., SSM_WIDTH + SSM_GROUPS * SSM_STATE:].reshape(bsz, t, SSM_GROUPS, SSM_STATE)
    dt = jax.nn.softplus(dt_raw.astype(jnp.float32).reshape(bsz, t, 2, SSM_HEADS) + dt_bias.astype(jnp.float32))
    a = -jnp.exp(a_log.astype(jnp.float32))
    y_f = ssd_chunked(xs, dt[:, :, 0], a[0], bm, cm)
    y_b = ssd_chunked(xs[:, ::-1], dt[:, ::-1, 1], a[1], bm[:, ::-1], cm[:, ::-1])[:, ::-1]
    y = y_f + y_b + d_skip.astype(jnp.float32)[:, None] * xs.astype(jnp.float32)
    y = y.reshape(bsz, t, SSM_WIDTH)
    return rms_norm(y * jax.nn.silu(z.astype(jnp.float32)), norm_g).astype(z.dtype)


def hybrid_mixer(x, w_in, conv_w, conv_b, dt_bias, a_log, d_skip, ssm_norm_g, attn_norm_g, rpb, w_out):
    bsz, t, _ = x.shape
    proj = x @ w_in
    q, k, v, z, xbc, dt_raw = jnp.split(proj, SPLITS, axis=-1)
    q = q.reshape(bsz, t, ATTN_HEADS, ATTN_HEAD_DIM) * (ATTN_HEAD_DIM ** -0.5)
    k = k.reshape(bsz, t, ATTN_HEADS, ATTN_HEAD_DIM)
    v = v.reshape(bsz, t, ATTN_HEADS, ATTN_HEAD_DIM)
    attn = rms_norm(neighbourhood_attention(q, k, v, rpb), attn_norm_g).astype(x.dtype)
    ssm = ssd_mixer(z, xbc, dt_raw, conv_w, conv_b, dt_bias, a_log, d_skip, ssm_norm_g)
    return jnp.concatenate([attn, ssm], axis=-1) @ w_out


def setup_inputs(seed: int = 0) -> dict:
    key = jax.random.key(seed)
    ks = jax.random.split(key, 26)
    nrm = lambda k, shape, s: jax.random.normal(k, shape, jnp.float32) * s
    dt0 = jnp.exp(jax.random.uniform(ks[10], (DEPTH, 2, SSM_HEADS), jnp.float32,
                                     minval=float(np.log(1e-3)), maxval=float(np.log(1e-1))))
    return {
        "x_prompt": nrm(ks[0], (BATCH, SEQ, D_MODEL), 1.0),
        "x_sample": nrm(ks[1], (DEC_BATCH, DEC_SEQ, D_MODEL), 1.0),
        "ffn1_w_gate": nrm(ks[2], (DEPTH, D_MODEL, D_FF), D_MODEL ** -0.5),
        "ffn1_w_up": nrm(ks[3], (DEPTH, D_MODEL, D_FF), D_MODEL ** -0.5),
        "ffn1_w_down": nrm(ks[4], (DEPTH, D_FF, D_MODEL), BETA * D_FF ** -0.5),
        "ln1_g": 1.0 + nrm(ks[5], (DEPTH, D_MODEL), 0.02),
        "ln1_b": nrm(ks[6], (DEPTH, D_MODEL), 0.02),
        "w_in": nrm(ks[7], (DEPTH, D_MODEL, D_IN_PROJ), D_MODEL ** -0.5),
        "conv_w": nrm(ks[8], (DEPTH, CONV_W, CONV_DIM), CONV_W ** -0.5),
        "conv_b": nrm(ks[9], (DEPTH, CONV_DIM), 0.01),
        "dt_bias": dt0 + jnp.log(-jnp.expm1(-dt0)),
        "a_log": jnp.log(jax.random.uniform(ks[11], (DEPTH, 2, SSM_HEADS), jnp.float32, minval=1.0, maxval=16.0)),
        "d_skip": 1.0 + nrm(ks[12], (DEPTH, SSM_HEADS), 0.02),
        "ssm_norm_g": 1.0 + nrm(ks[13], (DEPTH, SSM_WIDTH), 0.02),
        "attn_norm_g": 1.0 + nrm(ks[14], (DEPTH, ATTN_WIDTH), 0.02),
        "rpb": nrm(ks[15], (DEPTH, ATTN_HEADS, 2 * WIN_ROWS - 1, 2 * WIN_COLS - 1), 0.1),
        "w_out": nrm(ks[16], (DEPTH, D_MIX, D_MODEL), BETA * D_MIX ** -0.5),
        "ln2_g": 1.0 + nrm(ks[17], (DEPTH, D_MODEL), 0.02),
        "ln2_b": nrm(ks[18], (DEPTH, D_MODEL), 0.02),
        "ffn2_w_gate": nrm(ks[19], (DEPTH, D_MODEL, D_FF), D_MODEL ** -0.5),
        "ffn2_w_up": nrm(ks[20], (DEPTH, D_MODEL, D_FF), D_MODEL ** -0.5),
        "ffn2_w_down": nrm(ks[21], (DEPTH, D_FF, D_MODEL), BETA * D_FF ** -0.5),
        "ln3_g": 1.0 + nrm(ks[22], (DEPTH, D_MODEL), 0.02),
        "ln3_b": nrm(ks[23], (DEPTH, D_MODEL), 0.02),
    }


def reference(x_prompt, x_sample, ffn1_w_gate, ffn1_w_up, ffn1_w_down, ln1_g, ln1_b, w_in, conv_w, conv_b,
              dt_bias, a_log, d_skip, ssm_norm_g, attn_norm_g, rpb, w_out, ln2_g, ln2_b,
              ffn2_w_gate, ffn2_w_up, ffn2_w_down, ln3_g, ln3_b):
    def trunk(x):
        for l in range(DEPTH):
            x = layer_norm(ALPHA * x + 0.5 * swiglu_ffn(x, ffn1_w_gate[l], ffn1_w_up[l], ffn1_w_down[l]),
                           ln1_g[l], ln1_b[l])
            mix = hybrid_mixer(x, w_in[l], conv_w[l], conv_b[l], dt_bias[l], a_log[l], d_skip[l],
                               ssm_norm_g[l], attn_norm_g[l], rpb[l], w_out[l])
            x = layer_norm(ALPHA * x + mix, ln2_g[l], ln2_b[l])
            x = layer_norm(ALPHA * x + 0.5 * swiglu_ffn(x, ffn2_w_gate[l], ffn2_w_up[l], ffn2_w_down[l]),
                           ln3_g[l], ln3_b[l])
        return x

    y_prompt = trunk(x_prompt)
    y_sample = trunk(x_sample)
    return (y_prompt, y_sample)
```

```python
import numpy as np
import concourse.bass as bass
import concourse.mybir as mybir
from concourse.ap import AP
from concourse.bass_utils import run_bass_kernel_spmd

F32 = mybir.dt.float32
BF16 = mybir.dt.bfloat16
AF = mybir.ActivationFunctionType
ALU = mybir.AluOpType

D = 1024
DFF = 2816
NDC = 8
NFC = 22
NIN = 3088
TT = 512
DEPTH = 2
ALPHA = (2 * DEPTH) ** 0.25
LN_EPS = 1e-5
RMS_EPS = 1e-6
GW = 64

PRM_LN = 0
PRM_CONVW = 48
PRM_CONVB = 88
PRM_ATTNG = 96
PRM_N = 100
BC_DTBIAS = 0
BC_ALOG = 16
BC_DSKIP = 32
BC_SSMG = 40
BC_N = 552
C_ID, C_MLE, C_MGE, C_MLT, C_MGT, C_ONES, C_ONESD, C_ONES512, C_J2, C_BLK = range(10)
C_N = 10


def view(base, dims):
    return AP(base.tensor, base.offset, [list(base.ap[0])] + [list(d) for d in dims])


class Buf:
    __slots__ = ("w", "wk", "r", "bank")

    def __init__(self, bank=None):
        self.w = None
        self.wk = None
        self.r = {}
        self.bank = bank


class Eng:
    def __init__(self, nc, name, eng, has_sem=True):
        self.name = name
        self.eng = eng
        self.sem = nc.alloc_semaphore("s_" + name) if has_sem else None
        self.cnt = 0
        self.seen = {}


class Lane:
    def __init__(self, nc, name):
        self.sem = nc.alloc_semaphore("l_" + name)
        self.cnt = 0


class Prog:
    def __init__(self, nc):
        self.nc = nc
        self.pe = Eng(nc, "pe", nc.tensor)
        self.act = Eng(nc, "act", nc.scalar)
        self.dve = Eng(nc, "dve", nc.vector)
        self.pool = Eng(nc, "pool", nc.gpsimd)
        self.sp = Eng(nc, "sp", nc.sync, has_sem=False)
        self.engs = [self.pe, self.act, self.dve, self.pool, self.sp]
        self.lanes = []
        self.nwait = 0
        self.nins = 0
        self.dry = False

    def lane(self, name):
        l = Lane(self.nc, name)
        self.lanes.append(l)
        return l

    def _waits(self, E, reads, writes):
        need = {}

        def add(tok, k, raw):
            if tok is None:
                return
            s, v = tok
            if k is E:
                if E is self.pe:
                    return
            if need.get(s, 0) < v:
                need[s] = v

        for b in reads:
            add(b.w, b.wk, True)
            if b.bank is not None:
                add(b.bank.w, b.bank.wk, False)
        for b in writes:
            add(b.w, b.wk, False)
            for s, (v, k) in b.r.items():
                add((s, v), k, False)
            if b.bank is not None:
                add(b.bank.w, b.bank.wk, False)
        for s, v in need.items():
            if E.seen.get(s, 0) < v:
                E.eng.wait_ge(s, v)
                E.seen[s] = v
                self.nwait += 1

    def op(self, E, fn, reads=(), writes=(), inc=True):
        if self.dry:
            return None
        self._waits(E, reads, writes)
        ins = fn()
        self.nins += 1
        if inc:
            ins.then_inc(E.sem, 1)
            E.cnt += 1
            tok = (E.sem, E.cnt)
        else:
            tok = (E.sem, E.cnt + 1)
        for b in writes:
            b.w = tok
            b.wk = E
            b.r = {}
            if b.bank is not None:
                b.bank.w, b.bank.wk = tok, E
        for b in reads:
            b.r[E.sem] = (tok[1], E)
            if b.bank is not None:
                b.bank.w, b.bank.wk = tok, E
        return tok

    def dma(self, Q, lane, fn, reads=(), writes=()):
        if self.dry:
            return None
        self._waits(Q, reads, writes)
        ins = fn()
        self.nins += 1
        ins.then_inc(lane.sem, 16)
        lane.cnt += 16
        tok = (lane.sem, lane.cnt)
        for b in writes:
            b.w = tok
            b.wk = None
            b.r = {}
        for b in reads:
            b.r[lane.sem] = (tok[1], None)
        return tok

    def barrier(self, engs=None):
        for E in (engs or self.engs):
            for O in self.engs:
                if O.sem is not None and O is not E and O.cnt > 0 and E.seen.get(O.sem, 0) < O.cnt:
                    E.eng.wait_ge(O.sem, O.cnt)
                    E.seen[O.sem] = O.cnt
                    self.nwait += 1
            for l in self.lanes:
                if l.cnt > 0 and E.seen.get(l.sem, 0) < l.cnt:
                    E.eng.wait_ge(l.sem, l.cnt)
                    E.seen[l.sem] = l.cnt
                    self.nwait += 1


class Mem:
    def __init__(self, nc):
        self.nc = nc
        self.base = (nc.sbuf_base + 63) // 64 * 64
        self.top = nc.sbuf_top
        self.cur = self.base
        self.n = 0

    def alloc(self, shape, dt, name=None):
        esz = 4 if dt == F32 else 2
        per = esz
        for s in shape[1:]:
            per *= s
        off = self.cur
        self.cur = (off + per + 63) // 64 * 64
        assert self.cur <= self.top, ("SBUF overflow", name, self.cur, self.top)
        self.n += 1
        return self.nc.alloc_sbuf_tensor_at("%s_%d" % (name or "t", self.n), list(shape), dt, offset=off)

    def mark(self):
        return self.cur

    def reset(self, m):
        self.cur = m


class Cfg:
    def __init__(self, seg=2048, nseg=5):
        self.seg = seg
        self.nseg = nseg
        self.ntok = seg * nseg
        self.rs = seg // GW
        self.groups = [(0, 2)] + [(i, 1) for i in range(2, nseg)]


class Builder:
    def __init__(self, cfg, stages=("A", "B", "C"), depth=DEPTH):
        self.cfg = cfg
        self.depth = depth
        self.stages = stages
        nc = bass.Bass("TRN2", target_bir_lowering=False)
        self.nc = nc
        self.P = Prog(nc)
        self.M = Mem(nc)
        NT = cfg.ntok
        dt = nc.dram_tensor
        self.xT = dt("xT", [NDC, 128, NT], F32, kind="ExternalInput")
        self.yT = dt("yT", [NDC, 128, NT], F32, kind="ExternalOutput")
        self.wgu = dt("wgu", [depth, 2, NFC, 128, 2 * NDC * 128], F32, kind="ExternalInput")
        self.wd = dt("wd", [depth, 2, NDC, 128, NFC * 128], F32, kind="ExternalInput")
        self.win = dt("win", [depth, 128, NDC, NIN], F32, kind="ExternalInput")
        self.wout = dt("wout", [depth, 128, NDC, D], F32, kind="ExternalInput")
        self.prm = dt("prm", [depth, 128, PRM_N], F32, kind="ExternalInput")
        self.bc = dt("bc", [depth, 128, BC_N], F32, kind="ExternalInput")
        self.cst = dt("cst", [128, C_N, 128], F32, kind="ExternalInput")
        self.rpbf = dt("rpbf", [depth, 8, 15, 127], F32, kind="ExternalInput")
        self.cval = dt("cval", [128, 64], F32, kind="ExternalInput")
        self.percore = dt("percore", [128, 1 + 7 * 12], F32, kind="ExternalInput")
        self.x1 = dt("x1s", [NDC, 128, NT], F32)
        self.cat = dt("cats", [NDC, 128, NT], BF16)
        self.wgu_b = dt("wgub", [depth, 2, NFC, 128, 2 * NDC * 128], BF16)
        self.wd_b = dt("wdb", [depth, 2, NDC, 128, NFC * 128], BF16)
        self.conv_pending = []
        self.conv_lane = None
        self._setup_persistent()

    def _setup_persistent(self):
        nc, P, M = self.nc, self.P, self.M
        self.ps = [nc.alloc_psum_tensor("psb%d" % i, [128, 512], F32) for i in range(8)]
        self.bankb = [Buf() for _ in range(8)]
        self.psb = [Buf(self.bankb[i]) for i in range(8)]
        self.CST = M.alloc([128, C_N, 128], F32, "cst")
        self.CSTB = M.alloc([128, 2, 128], BF16, "cstb")
        self.PRM = M.alloc([128, self.depth, PRM_N], F32, "prm")
        self.BC = M.alloc([128, self.depth, BC_N], F32, "bc")
        self.AB = M.alloc([128, self.depth, 16], F32, "ab")
        self.PC = M.alloc([128, 1 + 84], F32, "pc")
        self.EPS = M.alloc([128, 4], F32, "eps")
        self.CVAL = M.alloc([128, 64], F32, "cval")
        b = self.bconst = Buf()
        ln = P.lane("const")
        sp = P.sp
        P.dma(sp, ln, lambda: nc.sync.dma_start(out=self.CST[:], in_=self.cst.ap()), writes=[b])
        P.dma(sp, ln, lambda: nc.sync.dma_start(out=self.PRM[:], in_=self.prm.ap().rearrange("l p n -> p l n")), writes=[b])
        P.dma(sp, ln, lambda: nc.sync.dma_start(out=self.BC[:], in_=self.bc.ap().rearrange("l p n -> p l n")), writes=[b])
        P.dma(sp, ln, lambda: nc.sync.dma_start(out=self.PC[:], in_=self.percore.ap()), writes=[b])
        P.dma(sp, ln, lambda: nc.sync.dma_start(out=self.CVAL[:], in_=self.cval.ap()), writes=[b])
        b2 = Buf()
        P.op(P.dve, lambda: nc.vector.tensor_copy(out=self.CSTB[:, 0, :], in_=self.CST[:, C_ID, :]), reads=[b], writes=[b2])
        P.op(P.dve, lambda: nc.vector.tensor_copy(out=self.CSTB[:, 1, :], in_=self.CST[:, C_BLK, :]), reads=[b], writes=[b2])
        P.op(P.dve, lambda: nc.vector.memset(self.EPS[:, 0:1], 4 * LN_EPS), writes=[b2])
        P.op(P.dve, lambda: nc.vector.memset(self.EPS[:, 1:2], LN_EPS), writes=[b2])
        P.op(P.dve, lambda: nc.vector.memset(self.EPS[:, 2:3], RMS_EPS), writes=[b2])
        P.op(P.dve, lambda: nc.vector.memset(self.EPS[:, 3:4], 1.0), writes=[b2])
        for l in range(self.depth):
            P.op(P.act, lambda l=l: nc.scalar.activation(out=self.AB[:, l, :], in_=self.BC[:, l, BC_ALOG:BC_ALOG + 16], func=AF.Exp),
                 reads=[b], writes=[b2])
            P.op(P.act, lambda l=l: nc.scalar.mul(out=self.AB[:, l, :], in_=self.AB[:, l, :], mul=-1.0), reads=[b2], writes=[b2])
        P.barrier()
        self.mark0 = M.mark()
        self.l_xbt = [P.lane("xbt0"), P.lane("xbt1")]
        self._lw = [P.lane("mw%d" % i) for i in range(3)]
        self._lcs = P.lane("cats")
        self.l_tt = P.lane("tt")

    def cm(self, i):
        return self.CST[:, i, :]

    def _setup_wstream(self):
        M, P = self.M, self.P
        self.NSLOT = 4
        self.wslot = [M.alloc([128, NFC * 128], BF16, "wslot") for _ in range(self.NSLOT)]
        self.wslot_b = [Buf() for _ in range(self.NSLOT)]
        if not hasattr(self, "wslot_l"):
            self.wslot_l = [P.lane("w%d" % i) for i in range(self.NSLOT)]
        self.wissued = 0
        self.wused = 0

    def _w_issue_upto(self, n):
        nc, P = self.nc, self.P
        while self.wissued < min(n, len(self.wreq)):
            i = self.wissued
            src, ncols = self.wreq[i]
            s = i % self.NSLOT
            P.dma(P.pool, self.wslot_l[s],
                  lambda s=s, src=src, ncols=ncols: nc.gpsimd.dma_start(out=self.wslot[s][:, 0:ncols], in_=src),
                  writes=[self.wslot_b[s]])
            self.wissued += 1

    def w_next(self, src, ncols):
        if self.P.dry:
            self.wreq.append((src, ncols))
            return 0
        i = self.wused
        self._w_issue_upto(i + self.NSLOT - 1)
        self.wused += 1
        return i % self.NSLOT

    def _setup_chain(self):
        M = self.M
        mk = lambda fn: [fn() for _ in range(2)]
        self.XT = mk(lambda: M.alloc([128, NDC, TT], F32, "XT"))
        self.XB = mk(lambda: M.alloc([128, NDC, TT], BF16, "XB"))
        self.HT = mk(lambda: M.alloc([128, NFC, TT], BF16, "HT"))
        self.SG = mk(lambda: M.alloc([128, TT], F32, "SG"))
        self.SQ = mk(lambda: [M.alloc([128, TT], F32, "SQ") for _ in range(2)])
        self.STA = mk(lambda: M.alloc([128, TT], F32, "STA"))
        self.STB = mk(lambda: M.alloc([128, TT], F32, "STB"))
        self.TMP = mk(lambda: [M.alloc([128, TT], F32, "TMP") for _ in range(2)])
        self.CATB = mk(lambda: M.alloc([128, NDC, TT], BF16, "CATB"))
        self.LA1 = mk(lambda: M.alloc([128, TT], F32, "LA1"))
        self.LA2 = mk(lambda: M.alloc([128, TT], F32, "LA2"))
        self.bLA1, self.bLA2 = mk(Buf), mk(Buf)
        self.WOUT = M.alloc([128, NDC, D], BF16, "WOUT")
        self.bXT = mk(lambda: [Buf() for _ in range(NDC)])
        self.bXB = mk(lambda: [Buf() for _ in range(NDC)])
        self.bHT = mk(lambda: [Buf() for _ in range(NFC)])
        self.bSG = mk(Buf)
        self.bSQ = mk(lambda: [Buf(), Buf()])
        self.bSTA, self.bSTB = mk(Buf), mk(Buf)
        self.bTMP = mk(lambda: [Buf(), Buf()])
        self.bCATB = mk(Buf)
        self.bWOUT = Buf()
        if not hasattr(self, "l_xt"):
            P = self.P
            self.l_xt = [P.lane("xt0"), P.lane("xt1")]
            self.l_xb = [P.lane("xb0"), P.lane("xb1")]
            self.l_cat = [P.lane("catb0"), P.lane("catb1")]
            self.l_wout = P.lane("wout")
            self.l_st = [P.lane("store0"), P.lane("store1")]
        self._setup_wstream()
        self.gu_ctr = 0
        self.y_ctr = 0

    def ffn(self, l, which, cx):
        nc, P = self.nc, self.P
        XT, XB, HT, SG = self.XT[cx], self.XB[cx], self.HT[cx], self.SG[cx]
        for j in range(NFC):
            g, u = ((0, 1), (2, 3))[self.gu_ctr % 2]
            self.gu_ctr += 1
            s = self.w_next((self.wgu if (l, which) == (0, 0) else self.wgu_b).ap()[l, which, j], 2 * NDC * 128)
            W = self.wslot[s]
            for c in range(NDC):
                P.op(P.pe, lambda: nc.tensor.matmul(self.ps[g][:], W[:, c * 128:(c + 1) * 128], XB[:, c, :],
                                                    start=(c == 0), stop=(c == NDC - 1)),
                     reads=[self.wslot_b[s], self.bXB[cx][c]], writes=[self.psb[g]], inc=(c == NDC - 1))
            for c in range(NDC):
                P.op(P.pe, lambda: nc.tensor.matmul(self.ps[u][:], W[:, (NDC + c) * 128:(NDC + c + 1) * 128], XB[:, c, :],
                                                    start=(c == 0), stop=(c == NDC - 1)),
                     reads=[self.wslot_b[s], self.bXB[cx][c]], writes=[self.psb[u]], inc=(c == NDC - 1))
            P.op(P.act, lambda: nc.scalar.activation(out=SG[:], in_=self.ps[g][:], func=AF.Silu),
                 reads=[self.psb[g]], writes=[self.bSG[cx]])
            P.op(P.dve, lambda: nc.vector.tensor_tensor(out=HT[:, j, :], in0=SG[:], in1=self.ps[u][:], op=ALU.mult),
                 reads=[self.bSG[cx], self.psb[u]], writes=[self.bHT[cx][j]])
            yield
        for i in range(NDC):
            y = 4 + (self.y_ctr % 2)
            self.y_ctr += 1
            s = self.w_next((self.wd if (l, which) == (0, 0) else self.wd_b).ap()[l, which, i], NFC * 128)
            W = self.wslot[s]
            for j in range(NFC):
                P.op(P.pe, lambda: nc.tensor.matmul(self.ps[y][:], W[:, j * 128:(j + 1) * 128], HT[:, j, :],
                                                    start=(j == 0), stop=(j == NFC - 1)),
                     reads=[self.wslot_b[s], self.bHT[cx][j]], writes=[self.psb[y]], inc=(j == NFC - 1))
            P.op(P.dve, lambda: nc.vector.scalar_tensor_tensor(out=XT[:, i, :], in0=XT[:, i, :], scalar=2.0 * ALPHA,
                                                               in1=self.ps[y][:], op0=ALU.mult, op1=ALU.add),
                 reads=[self.bXT[cx][i], self.psb[y]], writes=[self.bXT[cx][i]])
            yield

    def layernorm(self, l, which, eps_col, cx):
        nc, P = self.nc, self.P
        XT, XB, SQ, STA, STB, TMP = self.XT[cx], self.XB[cx], self.SQ[cx], self.STA[cx], self.STB[cx], self.TMP[cx]
        bXT, bXB, bSQ, bSTA, bSTB, bTMP = self.bXT[cx], self.bXB[cx], self.bSQ[cx], self.bSTA[cx], self.bSTB[cx], self.bTMP[cx]
        S1, S2 = 6, 7
        A1, A2, bA1, bA2 = self.LA1[cx], self.LA2[cx], self.bLA1[cx], self.bLA2[cx]
        for c in range(NDC):
            k = c % 2
            P.op(P.act, lambda: nc.scalar.activation(out=SQ[k][:], in_=XT[:, c, :], func=AF.Square),
                 reads=[bXT[c]], writes=[bSQ[k]])
            if c == 1:
                P.op(P.dve, lambda: nc.vector.tensor_tensor(out=A1[:], in0=XT[:, 0, :], in1=XT[:, 1, :], op=ALU.add),
                     reads=[bXT[0], bXT[1]], writes=[bA1])
                P.op(P.dve, lambda: nc.vector.tensor_tensor(out=A2[:], in0=SQ[0][:], in1=SQ[1][:], op=ALU.add),
                     reads=[bSQ[0], bSQ[1]], writes=[bA2])
            elif c > 1:
                P.op(P.dve, lambda: nc.vector.tensor_tensor(out=A1[:], in0=A1[:], in1=XT[:, c, :], op=ALU.add),
                     reads=[bA1, bXT[c]], writes=[bA1])
                P.op(P.dve, lambda: nc.vector.tensor_tensor(out=A2[:], in0=A2[:], in1=SQ[k][:], op=ALU.add),
                     reads=[bA2, bSQ[k]], writes=[bA2])
            if c % 4 == 3:
                yield
        P.op(P.pe, lambda: nc.tensor.matmul(self.ps[S1][:], self.cm(C_ONESD), A1[:], start=True, stop=True),
             reads=[bA1, self.bconst], writes=[self.psb[S1]])
        P.op(P.pe, lambda: nc.tensor.matmul(self.ps[S2][:], self.cm(C_ONESD), A2[:], start=True, stop=True),
             reads=[bA2, self.bconst], writes=[self.psb[S2]])
        P.op(P.act, lambda: nc.scalar.copy(out=STA[:], in_=self.ps[S1][:]), reads=[self.psb[S1]], writes=[bSTA])
        P.op(P.dve, lambda: nc.vector.tensor_tensor(out=STB[:], in0=STA[:], in1=STA[:], op=ALU.mult),
             reads=[bSTA], writes=[bSTB])
        P.op(P.dve, lambda: nc.vector.tensor_tensor(out=STB[:], in0=self.ps[S2][:], in1=STB[:], op=ALU.subtract),
             reads=[self.psb[S2], bSTB], writes=[bSTB])
        yield
        P.op(P.act, lambda: nc.scalar.activation(out=STB[:], in_=STB[:], func=AF.Ln, bias=self.EPS[:, eps_col:eps_col + 1]),
             reads=[bSTB], writes=[bSTB])
        P.op(P.act, lambda: nc.scalar.activation(out=STB[:], in_=STB[:], func=AF.Exp, scale=-0.5),
             reads=[bSTB], writes=[bSTB])
        P.op(P.dve, lambda: nc.vector.scalar_tensor_tensor(out=STA[:], in0=STA[:], scalar=-1.0, in1=STB[:],
                                                           op0=ALU.mult, op1=ALU.mult),
             reads=[bSTA, bSTB], writes=[bSTA])
        yield
        gcol = PRM_LN + which * 16
        for c in range(NDC):
            k = c % 2
            P.op(P.dve, lambda: nc.vector.tensor_tensor(out=TMP[k][:], in0=XT[:, c, :], in1=STB[:], op=ALU.mult),
                 reads=[bXT[c], bSTB], writes=[bTMP[k]])
            P.op(P.dve, lambda: nc.vector.tensor_tensor(out=TMP[k][:], in0=TMP[k][:], in1=STA[:], op=ALU.add),
                 reads=[bTMP[k], bSTA], writes=[bTMP[k]])
            P.op(P.act, lambda: nc.scalar.activation(out=XT[:, c, :], in_=TMP[k][:], func=AF.Identity,
                                                     scale=self.PRM[:, l, gcol + c:gcol + c + 1],
                                                     bias=self.PRM[:, l, gcol + 8 + c:gcol + 9 + c]),
                 reads=[bTMP[k], self.bconst], writes=[bXT[c]])
            P.op(P.act, lambda: nc.scalar.activation(out=XB[:, c, :], in_=TMP[k][:], func=AF.Identity,
                                                     scale=self.PRM[:, l, gcol + c:gcol + c + 1],
                                                     bias=self.PRM[:, l, gcol + 8 + c:gcol + 9 + c]),
                 reads=[bTMP[k], self.bconst], writes=[bXB[c]])
            if c % 2 == 1:
                yield

    def load_x_tile(self, src, t0, with_bf16, cx):
        nc, P = self.nc, self.P
        sv = src.ap()[:, :, t0:t0 + TT].rearrange("c p t -> p c t")
        P.dma(P.sp, self.l_xt[cx], lambda: nc.sync.dma_start(out=self.XT[cx][:], in_=sv), writes=self.bXT[cx])
        if with_bf16:
            P.dma(P.pool, self.l_xb[cx], lambda: nc.gpsimd.dma_start(out=self.XB[cx][:], in_=sv), writes=self.bXB[cx])

    def load_cat_tile(self, t0, cx):
        nc, P = self.nc, self.P
        cv = self.cat.ap()[:, :, t0:t0 + TT].rearrange("c p t -> p c t")
        P.dma(P.sp, self.l_cat[cx], lambda: nc.sync.dma_start(out=self.CATB[cx][:], in_=cv), writes=[self.bCATB[cx]])

    def store_x_tile(self, dst, t0, cx):
        nc, P = self.nc, self.P
        dv = dst.ap()[:, :, t0:t0 + TT].rearrange("c p t -> p c t")
        P.dma(P.sp, self.l_st[cx], lambda: nc.sync.dma_start(out=dv, in_=self.XT[cx][:]), reads=self.bXT[cx])

    def outproj(self, l, t0, cx):
        nc, P = self.nc, self.P
        XT, CATB = self.XT[cx], self.CATB[cx]
        for i in range(NDC):
            y = 4 + (self.y_ctr % 2)
            self.y_ctr += 1
            for c in range(NDC):
                P.op(P.pe, lambda: nc.tensor.matmul(self.ps[y][:], self.WOUT[:, c, i * 128:(i + 1) * 128], CATB[:, c, :],
                                                    start=(c == 0), stop=(c == NDC - 1)),
                     reads=[self.bWOUT, self.bCATB[cx]], writes=[self.psb[y]], inc=(c == NDC - 1))
            P.op(P.dve, lambda: nc.vector.scalar_tensor_tensor(out=XT[:, i, :], in0=XT[:, i, :], scalar=ALPHA,
                                                               in1=self.ps[y][:], op0=ALU.mult, op1=ALU.add),
                 reads=[self.bXT[cx][i], self.psb[y]], writes=[self.bXT[cx][i]])
            if i % 2 == 1:
                yield

    def _run_tiles(self, tile_gen, ntile):
        def drive():
            gens = {}
            nxt = 0
            steps = 0
            lead = None
            while nxt < ntile or gens:
                if nxt < ntile and len(gens) < 2 and (not gens or lead is None or steps >= lead):
                    cx = [c for c in (0, 1) if c not in gens][0]
                    gens[cx] = tile_gen(nxt, cx)
                    nxt += 1
                for cx in list(gens):
                    try:
                        next(gens[cx])
                    except StopIteration:
                        del gens[cx]
                steps += 1
                if lead is None and nxt == 1 and 0 not in gens:
                    lead = 0
                if lead is None and nxt == 1:
                    lead = self._half_steps
        P = self.P
        self.wreq = []
        P.dry = True
        drive()
        P.dry = False
        drive()

    def stage_A(self):
        m = self.M.mark()
        self._setup_chain()
        ntile = self.cfg.ntok // TT
        dst = self.x1 if "B" in self.stages or "C" in self.stages else self.yT
        self._half_steps = 20

        def tile(t, cx):
            self.load_x_tile(self.xT, t * TT, True, cx)
            for _ in range(self.NPRE):
                yield
            yield from self.ffn(0, 0, cx)
            yield from self.layernorm(0, 0, 0, cx)
            self.store_x_tile(dst, t * TT, cx)
        self._run_tiles(tile, ntile)
        self.P.barrier()
        self.M.reset(m)

    def stage_C(self, l):
        nc, P = self.nc, self.P
        m = self.M.mark()
        self._setup_chain()
        last = (l == self.depth - 1)
        ntile = self.cfg.ntok // TT
        P.dma(P.pool, self.l_wout, lambda: nc.gpsimd.dma_start(out=self.WOUT[:], in_=self.wout.ap()[l]), writes=[self.bWOUT])
        self._half_steps = 28

        def tile(t, cx):
            self.load_x_tile(self.x1, t * TT, False, cx)
            self.load_cat_tile(t * TT, cx)
            for _ in range(self.NPRE):
                yield
            yield from self.outproj(l, t * TT, cx)
            yield from self.layernorm(l, 1, 1, cx)
            yield from self.ffn(l, 1, cx)
            yield from self.layernorm(l, 2, 0, cx)
            if not last:
                yield from self.ffn(l + 1, 0, cx)
                yield from self.layernorm(l + 1, 0, 0, cx)
                self.store_x_tile(self.x1, t * TT, cx)
            else:
                self.store_x_tile(self.yT, t * TT, cx)
        self._run_tiles(tile, ntile)
        P.barrier()
        self.M.reset(m)

    def dbg_copy_in(self):
        nc, P = self.nc, self.P
        ln = P.lane("dbgin")
        b = Buf()
        for c in range(NDC):
            P.dma(P.sp, ln, lambda: nc.sync.dma_start(out=self.x1.ap()[c], in_=self.xT.ap()[c]), writes=[b])
        P.barrier()

    def dbg_cat_out(self):
        nc, P = self.nc, self.P
        m = self.M.mark()
        self._setup_chain()
        for t in range(self.cfg.ntok // TT):
            cv = self.cat.ap()[:, :, t * TT:(t + 1) * TT].rearrange("c p t -> p c t")
            P.dma(P.sp, self.l_cat[0], lambda: nc.sync.dma_start(out=self.CATB[0][:], in_=cv), writes=[self.bCATB[0]])
            P.op(P.dve, lambda: nc.vector.tensor_copy(out=self.XT[0][:], in_=self.CATB[0][:]), reads=[self.bCATB[0]], writes=self.bXT[0])
            self.store_x_tile(self.yT, t * TT, 0)
        P.barrier()
        self.M.reset(m)

    def build(self):
        if "Tin" in self.stages:
            self.dbg_copy_in()
            for l in range(self.depth if "B2" in self.stages else 1):
                self.stage_B(l)
            self.dbg_cat_out()
            self.P.barrier()
            return self.nc
        if "A" in self.stages:
            self.stage_A()
        for l in range(self.depth):
            if "B" in self.stages:
                self.stage_B(l)
            if "C" in self.stages:
                self.stage_C(l)
        self.P.barrier()
        return self.nc

    def stage_B(self, l):
        P = self.P
        m = self.M.mark()
        if l == 0 and "C" in self.stages:
            self._conv_plan()
        self._setup_mixer(l)
        m2 = self.M.mark()
        for (s0, ns) in self.cfg.groups:
            if "att" in self.mix_parts:
                self.attention(l, s0, ns)
                P.barrier()
                self.M.reset(m2)
            if "ssd" in self.mix_parts:
                self.ssd(l, s0, ns)
                P.barrier()
                self.M.reset(m2)
        if self.conv_pending:
            self._conv_issue(len(self.conv_pending))
            P.barrier()
        self.M.reset(m)

    mix_parts = ("att", "ssd")
    NPRE = 6

    def _setup_mixer(self, l):
        nc, P, M = self.nc, self.P, self.M
        self.TT = M.alloc([128, 4, 960], F32, "TT")
        self.bTT = Buf()
        G = M.alloc([128, 960], F32, "G")
        bG = Buf()
        ET = M.alloc([128, 512], F32, "ET")
        bET = Buf()
        for hp in range(4):
            for hpar in range(2):
                src = AP(self.rpbf, ((l * 8 + 2 * hp + hpar) * 15) * 127, [[1, 64], [127, 15], [1, 64]])
                P.dma(P.sp, self.l_tt, lambda: nc.sync.dma_start(out=G[hpar * 64:(hpar + 1) * 64, :].rearrange("p (r q) -> p r q", q=64), in_=src),
                      writes=[bG])
            for (c0, n) in ((0, 512), (512, 448)):
                P.op(P.pe, lambda: nc.tensor.matmul(self.ps[7][:, 0:n], self.cm(C_J2), G[:, c0:c0 + n], start=True, stop=True),
                     reads=[bG, self.bconst], writes=[self.psb[7]])
                P.op(P.act, lambda: nc.scalar.activation(out=ET[:, 0:n], in_=self.ps[7][:, 0:n], func=AF.Exp), reads=[self.psb[7]], writes=[bET])
                P.op(P.dve, lambda: nc.vector.tensor_tensor(out=self.TT[:, hp, c0:c0 + n].rearrange("p (r q) -> p r q", q=64),
                                                            in0=ET[:, 0:n].rearrange("p (r q) -> p r q", q=64),
                                                            in1=view(self.CVAL[:, 0:1], [[0, n // 64], [1, 64]]), op=ALU.mult),
                     reads=[bET, self.bconst], writes=[self.bTT])
        P.barrier()

    def _load_win(self, l, dst, c0, n, lane, buf):
        nc, P = self.nc, self.P
        P.dma(P.pool, lane, lambda: nc.gpsimd.dma_start(out=dst[:], in_=self.win.ap()[l, :, :, c0:c0 + n]), writes=[buf])

    def _conv_plan(self):
        for (l, w) in [(l, w) for l in range(self.depth) for w in range(2) if (l, w) != (0, 0)]:
            for j in range(NFC):
                self.conv_pending.append((self.wgu_b.ap()[l, w, j], self.wgu.ap()[l, w, j]))
            for i in range(NDC):
                self.conv_pending.append((self.wd_b.ap()[l, w, i], self.wd.ap()[l, w, i]))
        self.conv_lane = self.P.lane("wconv")
        self.bconv = Buf()

    def _conv_issue(self, n=1):
        nc, P = self.nc, self.P
        for _ in range(n):
            if not self.conv_pending:
                return
            dst, src = self.conv_pending.pop(0)
            P.dma(P.pool, self.conv_lane, lambda: nc.gpsimd.dma_start(out=dst, in_=src), writes=[self.bconv])

    def _load_xbt(self, tok0, slot):
        nc, P = self.nc, self.P
        sv = self.x1.ap()[:, :, tok0:tok0 + TT].rearrange("c p t -> p c t")
        P.dma(P.pool, self.l_xbt[slot], lambda: nc.gpsimd.dma_start(out=self.XBT[slot][:], in_=sv), writes=[self.bXBT[slot]])
        self._conv_issue(1)

    def _evac(self, k, out, in_, reads, writes):
        nc, P = self.nc, self.P
        if k % 2 == 0:
            P.op(P.act, lambda: nc.scalar.copy(out=out, in_=in_), reads=reads, writes=writes)
        else:
            P.op(P.dve, lambda: nc.vector.tensor_copy(out=out, in_=in_), reads=reads, writes=writes)

    def attention(self, l, s0, ns):
        nc, P, M, cfg = self.nc, self.P, self.M, self.cfg
        T = ns * cfg.seg
        R = T // GW
        tok0 = s0 * cfg.seg
        ntile = T // TT
        bd = (ns == 1)
        if bd:
            KT = M.alloc([128, 4, R, 128], BF16, "KBD")
            VB = M.alloc([128, R, 4, 128], BF16, "VBD")
        else:
            KT = M.alloc([128, 4, T], BF16, "KT")
            VB = M.alloc([128, R, 4, 64], BF16, "VB")
        QT = [M.alloc([128, 4, TT], BF16, "QT") for _ in range(2)]
        ATT = M.alloc([128, 4, TT], F32, "ATT")
        self.XBT = [M.alloc([128, NDC, TT], BF16, "XBT") for _ in range(2)]
        WQ = M.alloc([128, NDC, 512], BF16, "WQ")
        WK = M.alloc([128, NDC, 512], BF16, "WK")
        WV = M.alloc([128, NDC, 512], BF16, "WV")
        E = [M.alloc([128, 512], F32, "E") for _ in range(2)]
        PB = [M.alloc([128, 512], BF16, "PB") for _ in range(2)]
        PM = M.alloc([128, 512], F32, "PM")
        RD = [M.alloc([128, 64], F32, "RD") for _ in range(2)]
        SQa = [M.alloc([128, TT], F32, "SQa") for _ in range(2)]
        RSa = M.alloc([128, TT], F32, "RSa")
        TMa = [M.alloc([128, TT], F32, "TMa") for _ in range(2)]
        CATA = M.alloc([128, 4, TT], BF16, "CATA")
        bKT, bVB = Buf(), Buf()
        if bd:
            for hpar in range(2):
                hs = slice(hpar * 64, (hpar + 1) * 64)
                oc = slice((1 - hpar) * 64, (2 - hpar) * 64)
                P.op(P.pool, lambda: nc.gpsimd.memset(KT[hs, :, :, oc], 0.0), writes=[bKT])
                P.op(P.dve, lambda: nc.vector.memset(VB[hs, :, :, oc], 0.0), writes=[bVB])
        bQT = [Buf(), Buf()]
        bATT = [Buf() for _ in range(4)]
        self.bXBT = [Buf(), Buf()]
        bW = [Buf(), Buf(), Buf()]
        bE, bPB, bRD, bSQa, bTMa = [Buf(), Buf()], [Buf(), Buf()], [Buf(), Buf()], [Buf(), Buf()], [Buf(), Buf()]
        bPM, bRSa, bCATA = Buf(), Buf(), Buf()
        bOD = [Buf(self.bankb[3]), Buf(self.bankb[3])]
        self._load_win(l, WK, 512, 512, self._lw[0], bW[0])
        self._load_win(l, WV, 1024, 512, self._lw[1], bW[1])
        self._load_win(l, WQ, 0, 512, self._lw[2], bW[2])
        ev = 0
        for tt in range(ntile):
            sl = tt % 2
            self._load_xbt(tok0 + tt * TT, sl)
            X = self.XBT[sl]
            for hp in range(4):
                pb = 4 + (hp % 2)
                for c in range(NDC):
                    P.op(P.pe, lambda: nc.tensor.matmul(self.ps[pb][:], WK[:, c, hp * 128:(hp + 1) * 128], X[:, c, :],
                                                        start=(c == 0), stop=(c == NDC - 1)),
                         reads=[bW[0], self.bXBT[sl]], writes=[self.psb[pb]], inc=(c == NDC - 1))
                if bd:
                    for hpar in range(2):
                        hs = slice(hpar * 64, (hpar + 1) * 64)
                        self._evac(ev, KT[hs, hp, tt * 8:(tt + 1) * 8, hpar * 64:(hpar + 1) * 64],
                                   self.ps[pb][hs, :].rearrange("p (r k) -> p r k", k=64), [self.psb[pb]], [bKT]); ev += 1
                else:
                    self._evac(ev, KT[:, hp, tt * TT:(tt + 1) * TT], self.ps[pb][:], [self.psb[pb]], [bKT]); ev += 1
            for i in range(4):
                pb = 4 + (i % 2)
                for c in range(NDC):
                    P.op(P.pe, lambda: nc.tensor.matmul(self.ps[pb][:], X[:, c, i * 128:(i + 1) * 128], WV[:, c, :],
                                                        start=(c == 0), stop=(c == NDC - 1)),
                         reads=[bW[1], self.bXBT[sl]], writes=[self.psb[pb]], inc=(c == NDC - 1))
                for a in range(2):
                    row = tt * 8 + 2 * i + a
                    for hpar in range(2):
                        src = view(self.ps[pb][a * 64:(a + 1) * 64, hpar * 64:hpar * 64 + 1], [[128, 4], [1, 64]])
                        vdst = VB[hpar * 64:(hpar + 1) * 64, row, :, hpar * 64:(hpar + 1) * 64] if bd else VB[hpar * 64:(hpar + 1) * 64, row, :, :]
                        self._evac(ev, vdst, src, [self.psb[pb]], [bVB]); ev += 1
        special = pair_special(cfg.rs) if ns == 2 else {}
        sp_idx = {r: i for i, r in enumerate(sorted(special))}
        E4 = [E[0], E[1], M.alloc([128, 512], F32, "E"), M.alloc([128, 512], F32, "E")]
        bE4 = [bE[0], bE[1], Buf(), Buf()]
        PBA = [M.alloc([128, 8, 4, 64], BF16, "PBA") for _ in range(2)]
        PBXA = [M.alloc([128, 4, 4, 64], BF16, "PBXA") for _ in range(2)]
        PB8 = [[PBA[p][:, :, hp, :] for hp in range(4)] for p in range(2)]
        PBX = [[PBXA[p][:, :, hp, :] for hp in range(4)] for p in range(2)]
        bPB8 = [[Buf() for _ in range(4)] for _ in range(2)]
        bPBX = [[Buf() for _ in range(4)] for _ in range(2)]
        RD2 = [M.alloc([128, 256], F32, "RD2") for _ in range(2)]
        bRD2 = [Buf(), Buf()]
        ATT2 = [ATT, M.alloc([128, 4, TT], F32, "ATT2")]
        bATT2 = [[Buf() for _ in range(4)] for _ in range(2)]
        bODr = [Buf(self.bankb[4]), Buf(self.bankb[5])]
        evq = [0]

        def rows_of(r):
            rows = special[r][0] if r in special else key_rows(r, R)
            return [rows[0:8]] + ([rows[8:]] if len(rows) > 8 else [])

        def q_proj(rb):
            sl = rb % 2
            X = self.XBT[sl]
            for hp in range(4):
                for c in range(NDC):
                    P.op(P.pe, lambda: nc.tensor.matmul(self.ps[6][:], WQ[:, c, hp * 128:(hp + 1) * 128], X[:, c, :],
                                                        start=(c == 0), stop=(c == NDC - 1)),
                         reads=[bW[2], self.bXBT[sl]], writes=[self.psb[6]], inc=(c == NDC - 1))
                self._evac(evq[0], QT[sl][:, hp, :], self.ps[6][:], [self.psb[6]], [bQT[sl]]); evq[0] += 1

        def stage_S(r):
            rb, rl = divmod(r, 8)
            qs = rb % 2
            par = r % 2
            chunks = rows_of(r)
            for hp in range(4):
                for ci, rc in enumerate(chunks):
                    sb = hp if ci == 0 else 7
                    n = len(rc) * 64
                    for j, kr in enumerate(rc):
                        if bd:
                            P.op(P.pe, lambda: nc.tensor.matmul(self.ps[sb][:, j * 64:(j + 1) * 64], KT[:, hp, kr, :],
                                                                QT[qs][:, hp, rl * 64:(rl + 1) * 64], start=True, stop=True),
                                 reads=[bKT, bQT[qs]], writes=[self.psb[sb]], inc=(j == len(rc) - 1))
                            continue
                        for hpar in range(2):
                            hs = slice(hpar * 64, (hpar + 1) * 64)
                            last = (j == len(rc) - 1 and hpar == 1)
                            P.op(P.pe, lambda: nc.tensor.matmul(self.ps[sb][hs, j * 64:(j + 1) * 64], KT[hs, hp, kr * 64:(kr + 1) * 64],
                                                                QT[qs][hs, hp, rl * 64:(rl + 1) * 64], start=True, stop=True),
                                 reads=[bKT, bQT[qs]], writes=[self.psb[sb]], inc=last)
                    P.op(P.act, lambda: nc.scalar.activation(out=E4[hp][:, 0:n], in_=self.ps[sb][:, 0:n], func=AF.Exp, scale=0.125),
                         reads=[self.psb[sb]], writes=[bE4[hp]])
                    m0 = rc[0] - r + 7
                    assert 0 <= m0 and m0 + len(rc) <= 15, (r, rc)
                    dst, dbuf = (PB8[par][hp], bPB8[par][hp]) if ci == 0 else (PBX[par][hp], bPBX[par][hp])
                    if r in special:
                        P.op(P.dve, lambda: nc.vector.tensor_tensor(out=PM[:, 0:n], in0=E4[hp][:, 0:n], in1=self.TT[:, hp, m0 * 64:m0 * 64 + n],
                                                                    op=ALU.mult), reads=[bE4[hp], self.bTT], writes=[bPM])
                        col = 1 + sp_idx[r] * 12 + (0 if ci == 0 else 8)
                        P.op(P.dve, lambda: nc.vector.tensor_tensor(out=dst[:, 0:len(rc), :],
                                                                    in0=PM[:, 0:n].rearrange("p (j q) -> p j q", q=64),
                                                                    in1=view(self.PC[:, col:col + 1], [[1, len(rc)], [0, 64]]), op=ALU.mult),
                             reads=[bPM, self.bconst], writes=[dbuf])
                    else:
                        P.op(P.dve, lambda: nc.vector.tensor_tensor(out=dst[:, 0:len(rc), :], in0=E4[hp][:, 0:n].rearrange("p (j q) -> p j q", q=64),
                                                                    in1=self.TT[:, hp, m0 * 64:m0 * 64 + n].rearrange("p (j q) -> p j q", q=64),
                                                                    op=ALU.mult), reads=[bE4[hp], self.bTT], writes=[dbuf])

        def stage_V(r):
            rb, rl = divmod(r, 8)
            par = r % 2
            ab = rb % 2
            chunks = rows_of(r)
            ob = 4 + par
            tot = sum(len(rc) for rc in chunks)
            srcs = lambda hp: [(PB8[par][hp], bPB8[par][hp]), (PBX[par][hp], bPBX[par][hp])]
            for hp in range(4):
                g = 0
                for ci, rc in enumerate(chunks):
                    Pt, bP = srcs(hp)[ci]
                    for j, kr in enumerate(rc):
                        if bd:
                            P.op(P.pe, lambda: nc.tensor.matmul(self.ps[ob][:, hp * 64:(hp + 1) * 64], VB[:, kr, hp, :], Pt[:, j, :],
                                                                start=(g == 0), stop=(g == tot - 1)),
                                 reads=[bVB, bP], writes=[bODr[par]], inc=False)
                            g += 1
                            continue
                        for hpar in range(2):
                            hs = slice(hpar * 64, (hpar + 1) * 64)
                            P.op(P.pe, lambda: nc.tensor.matmul(self.ps[ob][hs, hp * 64:(hp + 1) * 64], VB[hs, kr, hp, :], Pt[hs, j, :],
                                                                start=(g == 0), stop=(g == tot - 1)),
                                 reads=[bVB, bP], writes=[bODr[par]], inc=False)
                        g += 1
            g = 0
            for ci, rc in enumerate(chunks):
                PA = (PBA, PBXA)[ci][par]
                bPs = [srcs(hp)[ci][1] for hp in range(4)]
                for j, kr in enumerate(rc):
                    P.op(P.pe, lambda: nc.tensor.matmul(self.ps[ob][:, 256:512], self.CSTB[:, 1, :], PA[:, j, :, :],
                                                        start=(g == 0), stop=(g == tot - 1)),
                         reads=bPs + [self.bconst], writes=[bODr[par]], inc=(g == tot - 1))
                    g += 1
            P.op(P.dve, lambda: nc.vector.reciprocal(out=RD2[par][:], in_=self.ps[ob][:, 256:512]), reads=[bODr[par]], writes=[bRD2[par]])
            P.op(P.dve, lambda: nc.vector.tensor_tensor(out=ATT2[ab][:, :, rl * 64:(rl + 1) * 64],
                                                        in0=self.ps[ob][:, 0:256].rearrange("p (h q) -> p h q", q=64),
                                                        in1=RD2[par][:].rearrange("p (h q) -> p h q", q=64), op=ALU.mult),
                 reads=[bODr[par], bRD2[par]], writes=bATT2[ab])

        def finish_block(rb):
            ab = rb % 2
            A = ATT2[ab]
            for c in range(4):
                kk = c % 2
                P.op(P.act, lambda: nc.scalar.activation(out=SQa[kk][:], in_=A[:, c, :], func=AF.Square), reads=[bATT2[ab][c]], writes=[bSQa[kk]])
                P.op(P.pe, lambda: nc.tensor.matmul(self.ps[6][:], self.cm(C_ONES512), SQa[kk][:], start=(c == 0), stop=(c == 3)),
                     reads=[bSQa[kk], self.bconst], writes=[self.psb[6]])
            P.op(P.act, lambda: nc.scalar.activation(out=RSa[:], in_=self.ps[6][:], func=AF.Ln, bias=self.EPS[:, 2:3]),
                 reads=[self.psb[6]], writes=[bRSa])
            P.op(P.act, lambda: nc.scalar.activation(out=RSa[:], in_=RSa[:], func=AF.Exp, scale=-0.5), reads=[bRSa], writes=[bRSa])
            for c in range(4):
                kk = c % 2
                P.op(P.dve, lambda: nc.vector.tensor_tensor(out=TMa[kk][:], in0=A[:, c, :], in1=RSa[:], op=ALU.mult),
                     reads=[bATT2[ab][c], bRSa], writes=[bTMa[kk]])
                P.op(P.act, lambda: nc.scalar.activation(out=CATA[:, c, :], in_=TMa[kk][:], func=AF.Identity,
                                                         scale=self.PRM[:, l, PRM_ATTNG + c:PRM_ATTNG + c + 1]),
                     reads=[bTMa[kk], self.bconst], writes=[bCATA])
            dv = self.cat.ap()[0:4, :, tok0 + rb * TT:tok0 + (rb + 1) * TT].rearrange("c p t -> p c t")
            P.dma(P.sp, self._lcs, lambda: nc.sync.dma_start(out=dv, in_=CATA[:]), reads=[bCATA])

        self._load_xbt(tok0, 0)
        if ntile > 1:
            self._load_xbt(tok0 + TT, 1)
        q_proj(0)
        stage_S(0)
        for r in range(R):
            rb, rl = divmod(r, 8)
            if r + 1 < R:
                if (r + 1) % 8 == 0:
                    q_proj(rb + 1)
                    if rb + 2 < ntile:
                        self._load_xbt(tok0 + (rb + 2) * TT, rb % 2)
                stage_S(r + 1)
            stage_V(r)
            if rl == 7:
                finish_block(rb)

    def ssd(self, l, s0, ns):
        nc, P, M, cfg = self.nc, self.P, self.M, self.cfg
        SEG = cfg.seg
        T = ns * SEG
        tok0 = s0 * SEG
        ntile = T // TT
        NCH = T // 128
        CPS = SEG // 128
        XST = M.alloc([128, 4, T], BF16, "XST")
        BTt = M.alloc([128, 2, T], BF16, "BTt")
        CTt = M.alloc([128, 2, T], BF16, "CTt")
        self.XBT = [M.alloc([128, NDC, TT], BF16, "XBT") for _ in range(2)]
        self.bXBT = [Buf(), Buf()]
        bXST, bBT, bCT = Buf(), Buf(), Buf()
        flag = self.PC[:, 0:1]
        mk = M.mark()
        WX = M.alloc([128, NDC, 1024], BF16, "WX")
        U = M.alloc([128, 8, SEG + 4], F32, "U")
        ACC = [M.alloc([128, TT], F32, "ACC") for _ in range(2)]
        HALO = M.alloc([128, 8, 4], F32, "HALO")
        XH = M.alloc([128, NDC, 4], BF16, "XH")
        bWX, bACC, bHALO, bXH = Buf(), [Buf(), Buf()], Buf(), Buf()
        bU = [[Buf() for _ in range(8)] for _ in range(SEG // TT)]
        self._load_win(l, WX, 2048, 1024, self._lw[0], bWX)
        ev = 0
        xl = 0
        tps = SEG // TT
        if ns == 2:
            hv = self.x1.ap()[:, :, tok0 + SEG - 2:tok0 + SEG + 2].rearrange("c p t -> p c t")
            P.dma(P.pool, self._lw[1], lambda: nc.gpsimd.dma_start(out=XH[:], in_=hv), writes=[bXH])
            for cc in range(8):
                for c in range(NDC):
                    P.op(P.pe, lambda: nc.tensor.matmul(self.ps[6][:, cc * 4:cc * 4 + 4], WX[:, c, cc * 128:(cc + 1) * 128], XH[:, c, :],
                                                        start=(c == 0), stop=(c == NDC - 1)),
                         reads=[bWX, bXH], writes=[self.psb[6]], inc=(c == NDC - 1))
            P.op(P.dve, lambda: nc.vector.tensor_scalar(out=HALO[:], in0=self.ps[6][:, 0:32].rearrange("p (c t) -> p c t", t=4),
                                                        scalar1=flag, scalar2=None, op0=ALU.mult),
                 reads=[self.psb[6], self.bconst], writes=[bHALO])
        for sg in range(ns):
            if sg == 0:
                P.op(P.dve, lambda: nc.vector.memset(U[:, :, 0:2], 0.0), writes=bU[0])
            else:
                P.op(P.dve, lambda: nc.vector.tensor_copy(out=U[:, :, 0:2], in_=HALO[:, :, 0:2]), reads=[bHALO], writes=bU[0])
            if sg == ns - 1:
                P.op(P.dve, lambda: nc.vector.memset(U[:, :, SEG + 2:SEG + 4], 0.0), writes=bU[-1])
            else:
                P.op(P.dve, lambda: nc.vector.tensor_copy(out=U[:, :, SEG + 2:SEG + 4], in_=HALO[:, :, 2:4]), reads=[bHALO], writes=bU[-1])
            def conv_tile(tl):
                tt = sg * tps + tl
                lo = tl * TT
                for cc in range(8):
                    wc = PRM_CONVW + cc * 5
                    k = cc % 2
                    ub = [bU[t][cc] for t in (tl - 1, tl, tl + 1) if 0 <= t < tps]
                    P.op(P.act, lambda: nc.scalar.activation(out=ACC[k][:], in_=U[:, cc, lo:lo + TT], func=AF.Identity,
                                                             scale=self.PRM[:, l, wc:wc + 1], bias=self.PRM[:, l, PRM_CONVB + cc:PRM_CONVB + cc + 1]),
                         reads=ub + [self.bconst], writes=[bACC[k]])
                    for kk in range(1, 5):
                        P.op(P.dve, lambda: nc.vector.scalar_tensor_tensor(out=ACC[k][:], in0=U[:, cc, lo + kk:lo + kk + TT],
                                                                           scalar=self.PRM[:, l, wc + kk:wc + kk + 1], in1=ACC[k][:],
                                                                           op0=ALU.mult, op1=ALU.add),
                             reads=ub + [bACC[k], self.bconst], writes=[bACC[k]])
                    if cc < 4:
                        dst, db = XST[:, cc, tt * TT:(tt + 1) * TT], bXST
                    elif cc < 6:
                        dst, db = BTt[:, cc - 4, tt * TT:(tt + 1) * TT], bBT
                    else:
                        dst, db = CTt[:, cc - 6, tt * TT:(tt + 1) * TT], bCT
                    P.op(P.act, lambda: nc.scalar.activation(out=dst, in_=ACC[k][:], func=AF.Silu), reads=[bACC[k]], writes=[db])

            for tl in range(tps):
                tt = sg * tps + tl
                sl = xl % 2
                xl += 1
                self._load_xbt(tok0 + tt * TT, sl)
                X = self.XBT[sl]
                for cc in range(8):
                    pb = 4 + (cc % 4)
                    for c in range(NDC):
                        P.op(P.pe, lambda: nc.tensor.matmul(self.ps[pb][:], WX[:, c, cc * 128:(cc + 1) * 128], X[:, c, :],
                                                            start=(c == 0), stop=(c == NDC - 1)),
                             reads=[bWX, self.bXBT[sl]], writes=[self.psb[pb]], inc=(c == NDC - 1))
                    self._evac(ev, U[:, cc, 2 + tl * TT:2 + (tl + 1) * TT], self.ps[pb][:], [self.psb[pb]], [bU[tl][cc]]); ev += 1
                if tl > 0:
                    conv_tile(tl - 1)
            conv_tile(tps - 1)
        P.barrier()
        M.reset(mk)
        PREVF = M.alloc([128, NCH, 512], BF16, "PREVF")
        bPREVF = [Buf() for _ in range(NCH)]
        if getattr(self, "ssd_stop", "") == "p0":
            return
        WZ = M.alloc([128, NDC, 512], BF16, "WZ")
        WDT = M.alloc([128, NDC, 16], BF16, "WDT")
        bWZ, bWDT = Buf(), Buf()
        self._load_win(l, WZ, 1536, 512, self._lw[1], bWZ)
        self._load_win(l, WDT, 3072, 16, self._lw[2], bWDT)
        f32t = lambda n, nm: M.alloc([128, n], F32, nm)
        HF, HB = f32t(512, "HF"), f32t(512, "HB")
        bHF, bHB = Buf(), Buf()
        DTA = M.alloc([128, NCH, 16], F32, "DTA")
        LAA = M.alloc([128, NCH, 16], F32, "LAA")
        bDTA = Buf()

        class Ctx:
            pass

        def mkctx():
            k = Ctx()
            k.DTB, k.EX, k.DT, k.LA, k.EXP, k.W8 = f32t(16, "DTB"), f32t(16, "EX"), f32t(16, "DT"), f32t(16, "LA"), f32t(32, "EXP"), f32t(16, "W8")
            k.XS = f32t(512, "XS")
            k.BTM = M.alloc([128, 256], BF16, "BTM")
            k.XD = [M.alloc([128, 512], BF16, "XD") for _ in range(2)]
            k.XDS = M.alloc([128, 512], BF16, "XDS")
            k.ZS = f32t(512, "ZS")
            k.CBM = [M.alloc([128, 2, 128], F32, "CBM") for _ in range(2)]
            k.RL = [f32t(512, "RL") for _ in range(4)]
            k.EE = [f32t(512, "EE") for _ in range(4)]
            k.MT = [M.alloc([128, 4, 128], BF16, "MT") for _ in range(4)]
            k.PREVB = M.alloc([128, 512], BF16, "PREVB")
            k.TMPY, k.YT, k.JUNK = f32t(512, "TMPY"), f32t(512, "YT"), f32t(512, "JUNK")
            k.SS, k.RS = f32t(1, "SS"), f32t(1, "RS")
            k.YN = M.alloc([128, 512], BF16, "YN")
            k.CATS = M.alloc([128, 4, 128], BF16, "CATS")
            for nm in ("DTB", "EX", "DT", "LA", "EXP", "W8", "XS", "BTM", "XDS", "ZS", "PREVB", "TMPY", "YT", "JUNK",
                       "SS", "RS", "YN", "CATS"):
                setattr(k, "b" + nm, Buf())
            k.bRL, k.bEE, k.bMT = [Buf() for _ in range(4)], [Buf() for _ in range(4)], [Buf() for _ in range(4)]
            k.bXD = [Buf(), Buf()]
            k.bCBM = [Buf(), Buf()]
            return k

        NCX = 2 if (M.top - M.cur) >= 2 * 44500 else 1
        ctxs = [mkctx() for _ in range(NCX)]
        p_dt, p_cu = self.ps[0][:, 0:16], self.ps[0][:, 16:64]
        p_tr = self.ps[1][:].bitcast(BF16)
        p_z, p_st, p_yo = self.ps[2], self.ps[2], self.ps[7]
        p_dd = [self.ps[3], self.ps[5]]
        p_yd = [self.ps[6], self.ps[4]]
        p_cb = self.ps[4][:, 0:256]
        p_tro = self.ps[4][:, 256:512].bitcast(BF16)
        b_dt, b_cu, b_trx, b_trb, b_z, b_st, b_cb, b_tro, b_yo = (Buf(self.bankb[i]) for i in (0, 0, 1, 1, 2, 2, 4, 4, 7))
        b_dd = [Buf(self.bankb[3]), Buf(self.bankb[5])]
        b_yd = [Buf(self.bankb[6]), Buf(self.bankb[4])]
        IDB = self.CSTB[:, 0, :]
        bc = self.bconst
        dtb_bias = self.BC[:, l, BC_DTBIAS:BC_DTBIAS + 16]
        a_b = self.AB[:, l, :]
        dsk = self.BC[:, l, BC_DSKIP:BC_DSKIP + 8]
        ng = self.BC[:, l, BC_SSMG:BC_SSMG + 512]

        def bc_hp(ap8):
            return view(ap8, [[1, 8], [0, 64]])

        def v3(ap):
            return ap.rearrange("p (h d) -> p h d", d=64)

        xstate = {"xl": 0, "X": None, "sl": 0, "tile": -1}

        def xtile(c):
            t = c // 4
            if t != xstate["tile"]:
                sl = xstate["xl"] % 2
                xstate["xl"] += 1
                self._load_xbt(tok0 + t * TT, sl)
                xstate.update(tile=t, sl=sl, X=self.XBT[sl])
            return xstate["X"], xstate["sl"]

        def dt_all():
            pall = self.ps[0]
            ball = Buf(self.bankb[0])
            for c in range(NCH):
                X, sl = xtile(c)
                tk = slice((c % 4) * 128, (c % 4) * 128 + 128)
                for c8 in range(NDC):
                    P.op(P.pe, lambda: nc.tensor.matmul(pall[:, c * 16:(c + 1) * 16], X[:, c8, tk], WDT[:, c8, :], start=(c8 == 0), stop=(c8 == NDC - 1)),
                         reads=[self.bXBT[sl], bWDT], writes=[ball], inc=(c8 == NDC - 1))
            n = NCH * 16
            la2, dt2 = LAA[:].rearrange("p c s -> p (c s)"), DTA[:].rearrange("p c s -> p (c s)")
            P.op(P.dve, lambda: nc.vector.tensor_tensor(out=LAA[:], in0=pall[:, 0:n].rearrange("p (c s) -> p c s", s=16),
                                                        in1=view(dtb_bias, [[0, NCH], [1, 16]]), op=ALU.add), reads=[ball, bc], writes=[bDTA])
            P.op(P.act, lambda: nc.scalar.activation(out=dt2, in_=la2, func=AF.Exp), reads=[bDTA], writes=[bDTA])
            P.op(P.act, lambda: nc.scalar.activation(out=dt2, in_=dt2, func=AF.Ln, bias=self.EPS[:, 3:4]), reads=[bDTA, bc], writes=[bDTA])
            P.op(P.dve, lambda: nc.vector.tensor_tensor(out=LAA[:], in0=DTA[:], in1=view(a_b, [[0, NCH], [1, 16]]), op=ALU.mult),
                 reads=[bDTA, bc], writes=[bDTA])

        def prep(k, c, X, sl):
            k.DT, k.LA = DTA[:, c, :], LAA[:, c, :]
            k.bDT = k.bLA = bDTA
            cs = slice(c * 128, (c + 1) * 128)
            for cc in range(4):
                P.op(P.pe, lambda: nc.tensor.transpose(p_tr[:, cc * 128:(cc + 1) * 128], XST[:, cc, cs], IDB),
                     reads=[bXST, bc], writes=[b_trx], inc=(cc == 3))
            for g in range(2):
                P.op(P.pe, lambda: nc.tensor.transpose(p_tr[:, 512 + g * 128:512 + (g + 1) * 128], BTt[:, g, cs], IDB),
                     reads=[bBT, bc], writes=[b_trb], inc=(g == 1))
            P.op(P.act, lambda: nc.scalar.copy(out=k.XS[:], in_=p_tr[:, 0:512]), reads=[b_trx], writes=[k.bXS])
            P.op(P.dve, lambda: nc.vector.tensor_copy(out=k.BTM[:], in_=p_tr[:, 512:768]), reads=[b_trb], writes=[k.bBTM])

        def states(k, H, bH, cdcol):
            for g in range(2):
                P.op(P.pe, lambda: nc.tensor.matmul(p_st[:, g * 256:(g + 1) * 256], k.BTM[:, g * 128:(g + 1) * 128], k.XDS[:, g * 256:(g + 1) * 256],
                                                    start=True, stop=True),
                     reads=[k.bBTM, k.bXDS], writes=[b_st], inc=(g == 1))
            P.op(P.dve, lambda: nc.vector.tensor_tensor(out=v3(H[:]), in0=v3(H[:]), in1=bc_hp(k.EXP[:, cdcol:cdcol + 8]), op=ALU.mult),
                 reads=[bH, k.bEXP], writes=[bH])
            P.op(P.dve, lambda: nc.vector.tensor_tensor(out=H[:], in0=H[:], in1=p_st[:], op=ALU.add), reads=[bH, b_st], writes=[bH])

        def chunk1(c, k):
            X, sl = xtile(c)
            prep(k, c, X, sl)
            yield None
            P.op(P.pe, lambda: nc.tensor.matmul(p_cu[:, 0:8], self.cm(C_MGT), k.LA[:, 0:8], start=True, stop=True), reads=[k.bLA, bc], writes=[b_cu], inc=False)
            P.op(P.pe, lambda: nc.tensor.matmul(p_cu[:, 8:16], self.cm(C_ONES), k.LA[:, 0:8], start=True, stop=True), reads=[k.bLA, bc], writes=[b_cu])
            P.op(P.act, lambda: nc.scalar.activation(out=k.EXP[:, 0:16], in_=p_cu[:, 0:16], func=AF.Exp), reads=[b_cu], writes=[k.bEXP])
            P.op(P.dve, lambda: nc.vector.tensor_tensor(out=k.W8[:, 0:8], in0=k.DT[:, 0:8], in1=k.EXP[:, 0:8], op=ALU.mult), reads=[k.bDT, k.bEXP], writes=[k.bW8])
            P.op(P.dve, lambda: nc.vector.tensor_tensor(out=v3(k.XDS[:]), in0=v3(k.XS[:]), in1=bc_hp(k.W8[:, 0:8]), op=ALU.mult),
                 reads=[k.bXS, k.bW8], writes=[k.bXDS])
            yield "need_H"
            if c % CPS == 0:
                if c == 0:
                    P.op(P.dve, lambda: nc.vector.memset(HF[:], 0.0), writes=[bHF])
                else:
                    P.op(P.dve, lambda: nc.vector.tensor_scalar(out=HF[:], in0=HF[:], scalar1=flag, scalar2=None, op0=ALU.mult),
                         reads=[bHF, bc], writes=[bHF])
            P.op(P.act, lambda: nc.scalar.copy(out=PREVF[:, c, :], in_=HF[:]), reads=[bHF], writes=[bPREVF[c]])
            states(k, HF, bHF, 8)
            yield "done_H"

        def chunk2(c, k):
            X, sl = xtile(c)
            prep(k, c, X, sl)
            yield None
            tk = slice((c % 4) * 128, (c % 4) * 128 + 128)
            cs = slice(c * 128, (c + 1) * 128)
            for c8 in range(NDC):
                P.op(P.pe, lambda: nc.tensor.matmul(p_z[:], X[:, c8, tk], WZ[:, c8, :], start=(c8 == 0), stop=(c8 == NDC - 1)),
                     reads=[self.bXBT[sl], bWZ], writes=[b_z], inc=(c8 == NDC - 1))
            P.op(P.act, lambda: nc.scalar.activation(out=k.ZS[:], in_=p_z[:], func=AF.Silu), reads=[b_z], writes=[k.bZS])
            P.op(P.pe, lambda: nc.tensor.matmul(p_cu[:, 0:8], self.cm(C_MLE), k.LA[:, 0:8], start=True, stop=True), reads=[k.bLA, bc], writes=[b_cu], inc=False)
            P.op(P.pe, lambda: nc.tensor.matmul(p_cu[:, 8:16], self.cm(C_MGE), k.LA[:, 8:16], start=True, stop=True), reads=[k.bLA, bc], writes=[b_cu], inc=False)
            P.op(P.pe, lambda: nc.tensor.matmul(p_cu[:, 16:24], self.cm(C_MLT), k.LA[:, 8:16], start=True, stop=True), reads=[k.bLA, bc], writes=[b_cu], inc=False)
            P.op(P.pe, lambda: nc.tensor.matmul(p_cu[:, 24:32], self.cm(C_ONES), k.LA[:, 8:16], start=True, stop=True), reads=[k.bLA, bc], writes=[b_cu])
            P.op(P.act, lambda: nc.scalar.activation(out=k.EXP[:, 0:32], in_=p_cu[:, 0:32], func=AF.Exp), reads=[b_cu], writes=[k.bEXP])
            for d in range(2):
                P.op(P.dve, lambda: nc.vector.tensor_tensor(out=v3(k.XD[d][:]), in0=v3(k.XS[:]), in1=bc_hp(k.DT[:, d * 8:d * 8 + 8]), op=ALU.mult),
                     reads=[k.bXS, k.bDT], writes=[k.bXD[d]])
            P.op(P.dve, lambda: nc.vector.tensor_tensor(out=k.W8[:, 0:8], in0=k.DT[:, 8:16], in1=k.EXP[:, 16:24], op=ALU.mult), reads=[k.bDT, k.bEXP], writes=[k.bW8])
            P.op(P.dve, lambda: nc.vector.tensor_tensor(out=v3(k.XDS[:]), in0=v3(k.XS[:]), in1=bc_hp(k.W8[:, 0:8]), op=ALU.mult),
                 reads=[k.bXS, k.bW8], writes=[k.bXDS])
            yield "need_H"
            if c % CPS == CPS - 1:
                if c == NCH - 1:
                    P.op(P.dve, lambda: nc.vector.memset(HB[:], 0.0), writes=[bHB])
                else:
                    P.op(P.dve, lambda: nc.vector.tensor_scalar(out=HB[:], in0=HB[:], scalar1=flag, scalar2=None, op0=ALU.mult),
                         reads=[bHB, bc], writes=[bHB])
            P.op(P.act, lambda: nc.scalar.copy(out=k.PREVB[:], in_=HB[:]), reads=[bHB], writes=[k.bPREVB])
            states(k, HB, bHB, 24)
            yield "done_H"
            for g in range(2):
                P.op(P.pe, lambda: nc.tensor.matmul(p_cb[:, g * 128:(g + 1) * 128], BTt[:, g, cs], CTt[:, g, cs], start=True, stop=True),
                     reads=[bBT, bCT], writes=[b_cb], inc=(g == 1))
            for d, mi in ((0, C_MLE), (1, C_MGE)):
                P.op(P.dve, lambda: nc.vector.tensor_tensor(out=k.CBM[d][:], in0=p_cb.rearrange("p (g s) -> p g s", s=128),
                                                            in1=view(self.cm(mi)[:, 0:1], [[0, 2], [1, 128]]), op=ALU.mult),
                     reads=[b_cb, bc], writes=[k.bCBM[d]])
            P.op(P.dve, lambda: nc.vector.tensor_tensor(out=v3(k.YT[:]), in0=v3(k.XS[:]), in1=bc_hp(dsk), op=ALU.mult), reads=[k.bXS, bc], writes=[k.bYT])
            yield None
            dgs = [(d, g) for d in range(2) for g in range(2)]
            for q, (d, g) in enumerate(dgs):
                mtri = (C_MLE, C_MGE)[d]
                col = d * 8 + g * 4
                P.op(P.pool, lambda: nc.gpsimd.tensor_tensor(out=k.RL[q][:].rearrange("p (h s) -> p h s", s=128),
                                                             in0=view(self.cm(mtri)[:, 0:1], [[0, 4], [1, 128]]),
                                                             in1=view(k.LA[:, col:col + 1], [[1, 4], [0, 128]]), op=ALU.mult),
                     reads=[k.bLA, bc], writes=[k.bRL[q]])
            yield None
            for q, (d, g) in enumerate(dgs):
                mstrict = (C_MGT, C_MLT)[d]
                P.op(P.pe, lambda: nc.tensor.matmul(p_dd[q % 2][:], self.cm(mstrict), k.RL[q][:], start=True, stop=True),
                     reads=[k.bRL[q], bc], writes=[b_dd[q % 2]])
                P.op(P.act, lambda: nc.scalar.activation(out=k.EE[q][:], in_=p_dd[q % 2][:], func=AF.Exp), reads=[b_dd[q % 2]], writes=[k.bEE[q]])
                P.op(P.dve, lambda: nc.vector.tensor_tensor(out=k.MT[q][:], in0=k.EE[q][:].rearrange("p (h s) -> p h s", s=128),
                                                            in1=view(k.CBM[d][:, g, 0:1], [[0, 4], [1, 128]]), op=ALU.mult),
                     reads=[k.bEE[q], k.bCBM[d]], writes=[k.bMT[q]])
            for q, (d, g) in enumerate(dgs):
                for h in range(4):
                    hh = g * 4 + h
                    P.op(P.pe, lambda: nc.tensor.matmul(p_yd[d][:, hh * 64:(hh + 1) * 64], k.MT[q][:, h, :], k.XD[d][:, hh * 64:(hh + 1) * 64], start=True, stop=True),
                         reads=[k.bMT[q], k.bXD[d]], writes=[b_yd[d]], inc=(h == 3))
            for d in range(2):
                P.op(P.dve, lambda: nc.vector.tensor_tensor(out=k.YT[:], in0=k.YT[:], in1=p_yd[d][:], op=ALU.add), reads=[k.bYT, b_yd[d]], writes=[k.bYT])
            yield None
            for d in range(2):
                for g in range(2):
                    if d == 0:
                        P.op(P.pe, lambda: nc.tensor.matmul(p_yo[:, g * 256:(g + 1) * 256], CTt[:, g, cs], PREVF[:, c, g * 256:(g + 1) * 256], start=True, stop=True),
                             reads=[bCT, bPREVF[c]], writes=[b_yo], inc=(g == 1))
                    else:
                        P.op(P.pe, lambda: nc.tensor.matmul(p_yo[:, g * 256:(g + 1) * 256], CTt[:, g, cs], k.PREVB[:, g * 256:(g + 1) * 256], start=True, stop=True),
                             reads=[bCT, k.bPREVB], writes=[b_yo], inc=(g == 1))
                P.op(P.dve, lambda: nc.vector.tensor_tensor(out=v3(k.TMPY[:]), in0=v3(p_yo[:]), in1=bc_hp(k.EXP[:, d * 8:d * 8 + 8]), op=ALU.mult),
                     reads=[b_yo, k.bEXP], writes=[k.bTMPY])
                P.op(P.dve, lambda: nc.vector.tensor_tensor(out=k.YT[:], in0=k.YT[:], in1=k.TMPY[:], op=ALU.add), reads=[k.bYT, k.bTMPY], writes=[k.bYT])
            yield None
            P.op(P.dve, lambda: nc.vector.tensor_tensor(out=k.YT[:], in0=k.YT[:], in1=k.ZS[:], op=ALU.mult), reads=[k.bYT, k.bZS], writes=[k.bYT])
            P.op(P.act, lambda: nc.scalar.activation(out=k.JUNK[:], in_=k.YT[:], func=AF.Square, accum_out=k.SS[:]), reads=[k.bYT], writes=[k.bJUNK, k.bSS])
            P.op(P.act, lambda: nc.scalar.activation(out=k.RS[:], in_=k.SS[:], func=AF.Ln, scale=1.0 / 512.0, bias=self.EPS[:, 2:3]), reads=[k.bSS, bc], writes=[k.bRS])
            P.op(P.act, lambda: nc.scalar.activation(out=k.RS[:], in_=k.RS[:], func=AF.Exp, scale=-0.5), reads=[k.bRS], writes=[k.bRS])
            P.op(P.dve, lambda: nc.vector.scalar_tensor_tensor(out=k.YN[:], in0=k.YT[:], scalar=k.RS[:, 0:1], in1=ng, op0=ALU.mult, op1=ALU.mult),
                 reads=[k.bYT, k.bRS, bc], writes=[k.bYN])
            yield None
            for cc in range(4):
                P.op(P.pe, lambda: nc.tensor.transpose(p_tro[:, cc * 128:(cc + 1) * 128], k.YN[:, cc * 128:(cc + 1) * 128], IDB),
                     reads=[k.bYN, bc], writes=[b_tro], inc=(cc == 3))
            P.op(P.act, lambda: nc.scalar.copy(out=k.CATS[:], in_=p_tro.rearrange("p (c t) -> p c t", t=128)), reads=[b_tro], writes=[k.bCATS])
            dv = self.cat.ap()[4:8, :, tok0 + c * 128:tok0 + (c + 1) * 128].rearrange("c p t -> p c t")
            P.dma(P.sp, self._lcs, lambda: nc.sync.dma_start(out=dv, in_=k.CATS[:]), reads=[k.bCATS])

        def run_chunks(order, gen_fn, lag):
            pos = {c: i for i, c in enumerate(order)}
            active = []
            free = list(range(NCX))
            nxt = 0
            hdone = -1
            since = lag
            while nxt < len(order) or active:
                if nxt < len(order) and free and since >= lag:
                    cx = free.pop(0)
                    c = order[nxt]
                    nxt += 1
                    active.append([c, cx, gen_fn(c, ctxs[cx]), False])
                    since = 0
                for a in list(active):
                    if a[3]:
                        if hdone == pos[a[0]] - 1:
                            a[3] = False
                        else:
                            continue
                    try:
                        tag = next(a[2])
                    except StopIteration:
                        active.remove(a)
                        free.append(a[1])
                        continue
                    if tag == "need_H" and hdone != pos[a[0]] - 1:
                        a[3] = True
                    elif tag == "done_H":
                        hdone = pos[a[0]]
                since += 1

        if getattr(self, "ssd_stop", "") == "p0":
            return
        dt_all()
        xstate["tile"] = -1
        run_chunks(list(range(NCH)), chunk1, 1)
        xstate["tile"] = -1
        run_chunks(list(range(NCH - 1, -1, -1)), chunk2, 5)


def const_mats():
    c = np.zeros((128, C_N, 128), np.float32)
    i = np.arange(128)
    c[:, C_ID] = np.eye(128)
    c[:, C_MLE] = (i[:, None] <= i[None, :])
    c[:, C_MGE] = (i[:, None] >= i[None, :])
    c[:, C_MLT] = (i[:, None] < i[None, :])
    c[:, C_MGT] = (i[:, None] > i[None, :])
    c[:, C_ONES] = 1.0
    c[:, C_ONESD] = 1.0 / D
    c[:, C_ONES512] = 0.0
    j2 = np.zeros((128, 128), np.float32)
    for h in range(2):
        for k in range(64):
            j2[h * 64 + k, h * 64 + 63 - k] = 1.0
    c[:, C_J2] = j2
    blk = np.zeros((128, 128), np.float32)
    blk[:64, :64] = 1.0
    blk[64:, 64:] = 1.0
    c[:, C_BLK] = blk
    c[:, C_ONES512] = 1.0 / 512.0
    return c


def col_valid():
    qc = np.arange(GW)
    cstart = np.clip(qc - 8, 0, GW - 16)
    kc = np.arange(GW)
    v = (kc[:, None] >= cstart[None, :]) & (kc[:, None] < cstart[None, :] + 16)
    return np.concatenate([v, v], 0).astype(np.float32)


def key_rows(r, rows):
    rs = int(np.clip(r - 4, 0, rows - 8))
    return list(range(rs, rs + 8))


def pair_special(rs_):
    out = {}
    for r in range(2 * rs_):
        kj = key_rows(r, 2 * rs_)
        seg = r // rs_
        ku = [seg * rs_ + k for k in key_rows(r - seg * rs_, rs_)]
        if kj != ku:
            lo, hi = min(kj[0], ku[0]), max(kj[-1], ku[-1])
            out[r] = (list(range(lo, hi + 1)), kj, ku)
    return out


def prep_weights(inp, depth=DEPTH):
    f = lambda a: np.ascontiguousarray(a, dtype=np.float32)
    wgu = np.zeros((depth, 2, NFC, 128, 2, NDC, 128), np.float32)
    wd = np.zeros((depth, 2, NDC, 128, NFC, 128), np.float32)
    for l in range(depth):
        for w, pre in enumerate(("ffn1", "ffn2")):
            g = np.asarray(inp[pre + "_w_gate"][l]).reshape(NDC, 128, NFC, 128)
            u = np.asarray(inp[pre + "_w_up"][l]).reshape(NDC, 128, NFC, 128)
            wgu[l, w, :, :, 0] = g.transpose(2, 1, 0, 3)
            wgu[l, w, :, :, 1] = u.transpose(2, 1, 0, 3)
            dn = np.asarray(inp[pre + "_w_down"][l]).reshape(NFC, 128, NDC, 128)
            wd[l, w] = dn.transpose(2, 1, 0, 3)
    win = np.stack([np.asarray(inp["w_in"][l]).reshape(NDC, 128, NIN).transpose(1, 0, 2) for l in range(depth)])
    wout = np.stack([np.asarray(inp["w_out"][l]).reshape(NDC, 128, D).transpose(1, 0, 2) for l in range(depth)])
    prm = np.zeros((depth, 128, PRM_N), np.float32)
    bc = np.zeros((depth, 128, BC_N), np.float32)
    rp = np.zeros((depth, 8, 15, 127), np.float32)
    for l in range(depth):
        for k, nm in enumerate(("ln1", "ln2", "ln3")):
            prm[l, :, PRM_LN + k * 16:PRM_LN + k * 16 + 8] = np.asarray(inp[nm + "_g"][l]).reshape(NDC, 128).T
            prm[l, :, PRM_LN + k * 16 + 8:PRM_LN + k * 16 + 16] = np.asarray(inp[nm + "_b"][l]).reshape(NDC, 128).T
        cw = np.asarray(inp["conv_w"][l]).reshape(5, 8, 128)
        prm[l, :, PRM_CONVW:PRM_CONVW + 40] = cw.transpose(2, 1, 0).reshape(128, 40)
        prm[l, :, PRM_CONVB:PRM_CONVB + 8] = np.asarray(inp["conv_b"][l]).reshape(8, 128).T
        prm[l, :, PRM_ATTNG:PRM_ATTNG + 4] = np.asarray(inp["attn_norm_g"][l]).reshape(4, 128).T
        bc[l, :, BC_DTBIAS:BC_DTBIAS + 16] = np.asarray(inp["dt_bias"][l]).reshape(16)[None]
        bc[l, :, BC_ALOG:BC_ALOG + 16] = np.asarray(inp["a_log"][l]).reshape(16)[None]
        bc[l, :, BC_DSKIP:BC_DSKIP + 8] = np.asarray(inp["d_skip"][l])[None]
        bc[l, :, BC_SSMG:BC_SSMG + 512] = np.asarray(inp["ssm_norm_g"][l])[None]
        pad = np.zeros((8, 15, 127), np.float32)
        pad[:, :, 48:79] = np.asarray(inp["rpb"][l])
        rp[l] = pad[:, :, ::-1]
    return dict(wgu=f(wgu.reshape(depth, 2, NFC, 128, 2 * NDC * 128)), wd=f(wd.reshape(depth, 2, NDC, 128, NFC * 128)),
                win=f(win), wout=f(wout), prm=prm, bc=bc, rpbf=f(rp), cst=const_mats(), cval=col_valid())


def percore_arr(cfg, joined):
    a = np.zeros((128, 85), np.float32)
    a[:, 0] = 1.0 if joined else 0.0
    sp = pair_special(cfg.rs)
    for i, r in enumerate(sorted(sp)):
        union, kj, ku = sp[r]
        use = kj if joined else ku
        for j, kr in enumerate(union):
            a[:, 1 + i * 12 + j] = 1.0 if kr in use else 0.0
    return a


def to_fm(x):
    return np.ascontiguousarray(x.T.reshape(NDC, 128, x.shape[0]))


def from_fm(y):
    return np.ascontiguousarray(y.reshape(D, y.shape[2]).T)


_CACHE = {}


def kernel(**inputs):
    xp = np.asarray(inputs["x_prompt"], np.float32)
    xs = np.asarray(inputs["x_sample"], np.float32)
    cfg = Cfg(2048, 5)
    if "nc" not in _CACHE:
        _CACHE["nc"] = Builder(cfg).build()
    nc = _CACHE["nc"]
    w = prep_weights(inputs)
    in_maps = []
    assign = []
    for c in range(8):
        if c < 4:
            seqs = [("s", c)] + [("p", 3 * c + i) for i in range(3)]
        else:
            seqs = [("p", 12 + 5 * (c - 4) + i) for i in range(5)]
        assign.append(seqs)
        xc = np.concatenate([xs[i] if k == "s" else xp[i] for k, i in seqs], 0)
        m = dict(w)
        m["xT"] = to_fm(xc)
        m["percore"] = percore_arr(cfg, c < 4)
        in_maps.append(m)
    res = run_bass_kernel_spmd(nc, in_maps, core_ids=list(range(8)))
    yp = np.zeros_like(xp)
    ys = np.zeros_like(xs)
    for c in range(8):
        y = from_fm(np.asarray(res.results[c]["yT"]))
        o = 0
        for k, i in assign[c]:
            n = 4096 if k == "s" else 2048
            if k == "s":
                ys[i] = y[o:o + n]
            else:
                yp[i] = y[o:o + n]
            o += n
    return (yp, ys)
```

```python
import numpy as np
import concourse.bass as bass
import concourse.mybir as mybir
from concourse.ap import AP
from concourse.bass_utils import run_bass_kernel_spmd

F32 = mybir.dt.float32
BF16 = mybir.dt.bfloat16
AF = mybir.ActivationFunctionType
ALU = mybir.AluOpType

D = 1024
DFF = 2816
NDC = 8
NFC = 22
NIN = 3088
TT = 512
DEPTH = 2
ALPHA = (2 * DEPTH) ** 0.25
LN_EPS = 1e-5
RMS_EPS = 1e-6
GW = 64

PRM_LN = 0
PRM_CONVW = 48
PRM_CONVB = 88
PRM_ATTNG = 96
PRM_N = 100
BC_DTBIAS = 0
BC_ALOG = 16
BC_DSKIP = 32
BC_SSMG = 40
BC_N = 552
C_ID, C_MLE, C_MGE, C_MLT, C_MGT, C_ONES, C_ONESD, C_ONES512, C_J2, C_BLK = range(10)
C_N = 10


def view(base, dims):
    return AP(base.tensor, base.offset, [list(base.ap[0])] + [list(d) for d in dims])


class Buf:
    __slots__ = ("w", "wk", "r", "bank")

    def __init__(self, bank=None):
        self.w = None
        self.wk = None
        self.r = {}
        self.bank = bank


class Eng:
    def __init__(self, nc, name, eng, has_sem=True):
        self.name = name
        self.eng = eng
        self.sem = nc.alloc_semaphore("s_" + name) if has_sem else None
        self.cnt = 0
        self.seen = {}


class Lane:
    def __init__(self, nc, name):
        self.sem = nc.alloc_semaphore("l_" + name)
        self.cnt = 0


class Prog:
    def __init__(self, nc):
        self.nc = nc
        self.pe = Eng(nc, "pe", nc.tensor)
        self.act = Eng(nc, "act", nc.scalar)
        self.dve = Eng(nc, "dve", nc.vector)
        self.pool = Eng(nc, "pool", nc.gpsimd)
        self.sp = Eng(nc, "sp", nc.sync, has_sem=False)
        self.engs = [self.pe, self.act, self.dve, self.pool, self.sp]
        self.lanes = []
        self.nwait = 0
        self.nins = 0
        self.dry = False

    def lane(self, name):
        l = Lane(self.nc, name)
        self.lanes.append(l)
        return l

    def _waits(self, E, reads, writes):
        need = {}

        def add(tok, k, raw):
            if tok is None:
                return
            s, v = tok
            if k is E:
                if E is self.pe:
                    return
            if need.get(s, 0) < v:
                need[s] = v

        for b in reads:
            add(b.w, b.wk, True)
            if b.bank is not None:
                add(b.bank.w, b.bank.wk, False)
        for b in writes:
            add(b.w, b.wk, False)
            for s, (v, k) in b.r.items():
                add((s, v), k, False)
            if b.bank is not None:
                add(b.bank.w, b.bank.wk, False)
        for s, v in need.items():
            if E.seen.get(s, 0) < v:
                E.eng.wait_ge(s, v)
                E.seen[s] = v
                self.nwait += 1

    def op(self, E, fn, reads=(), writes=(), inc=True):
        if self.dry:
            return None
        self._waits(E, reads, writes)
        ins = fn()
        self.nins += 1
        if inc:
            ins.then_inc(E.sem, 1)
            E.cnt += 1
            tok = (E.sem, E.cnt)
        else:
            tok = (E.sem, E.cnt + 1)
        for b in writes:
            b.w = tok
            b.wk = E
            b.r = {}
            if b.bank is not None:
                b.bank.w, b.bank.wk = tok, E
        for b in reads:
            b.r[E.sem] = (tok[1], E)
            if b.bank is not None:
                b.bank.w, b.bank.wk = tok, E
        return tok

    def dma(self, Q, lane, fn, reads=(), writes=()):
        if self.dry:
            return None
        self._waits(Q, reads, writes)
        ins = fn()
        self.nins += 1
        ins.then_inc(lane.sem, 16)
        lane.cnt += 16
        tok = (lane.sem, lane.cnt)
        for b in writes:
            b.w = tok
            b.wk = None
            b.r = {}
        for b in reads:
            b.r[lane.sem] = (tok[1], None)
        return tok

    def barrier(self, engs=None):
        for E in (engs or self.engs):
            for O in self.engs:
                if O.sem is not None and O is not E and O.cnt > 0 and E.seen.get(O.sem, 0) < O.cnt:
                    E.eng.wait_ge(O.sem, O.cnt)
                    E.seen[O.sem] = O.cnt
                    self.nwait += 1
            for l in self.lanes:
                if l.cnt > 0 and E.seen.get(l.sem, 0) < l.cnt:
                    E.eng.wait_ge(l.sem, l.cnt)
                    E.seen[l.sem] = l.cnt
                    self.nwait += 1


class Mem:
    def __init__(self, nc):
        self.nc = nc
        self.base = (nc.sbuf_base + 63) // 64 * 64
        self.top = nc.sbuf_top
        self.cur = self.base
        self.n = 0

    def alloc(self, shape, dt, name=None):
        esz = 4 if dt == F32 else 2
        per = esz
        for s in shape[1:]:
            per *= s
        off = self.cur
        self.cur = (off + per + 63) // 64 * 64
        assert self.cur <= self.top, ("SBUF overflow", name, self.cur, self.top)
        self.n += 1
        return self.nc.alloc_sbuf_tensor_at("%s_%d" % (name or "t", self.n), list(shape), dt, offset=off)

    def mark(self):
        return self.cur

    def reset(self, m):
        self.cur = m


class Cfg:
    def __init__(self, seg=2048, nseg=5):
        self.seg = seg
        self.nseg = nseg
        self.ntok = seg * nseg
        self.rs = seg // GW
        self.groups = [(0, 2)] + [(i, 1) for i in range(2, nseg)]


class Builder:
    def __init__(self, cfg, stages=("A", "B", "C"), depth=DEPTH):
        self.cfg = cfg
        self.depth = depth
        self.stages = stages
        nc = bass.Bass("TRN2", target_bir_lowering=False)
        self.nc = nc
        self.P = Prog(nc)
        self.M = Mem(nc)
        NT = cfg.ntok
        dt = nc.dram_tensor
        self.xT = dt("xT", [NDC, 128, NT], F32, kind="ExternalInput")
        self.yT = dt("yT", [NDC, 128, NT], F32, kind="ExternalOutput")
        self.wgu = dt("wgu", [depth, 2, NFC, 128, 2 * NDC * 128], F32, kind="ExternalInput")
        self.wd = dt("wd", [depth, 2, NDC, 128, NFC * 128], F32, kind="ExternalInput")
        self.win = dt("win", [depth, 128, NDC, NIN], F32, kind="ExternalInput")
        self.wout = dt("wout", [depth, 128, NDC, D], F32, kind="ExternalInput")
        self.prm = dt("prm", [depth, 128, PRM_N], F32, kind="ExternalInput")
        self.bc = dt("bc", [depth, 128, BC_N], F32, kind="ExternalInput")
        self.cst = dt("cst", [128, C_N, 128], F32, kind="ExternalInput")
        self.rpbf = dt("rpbf", [depth, 8, 15, 127], F32, kind="ExternalInput")
        self.cval = dt("cval", [128, 64], F32, kind="ExternalInput")
        self.percore = dt("percore", [128, 1 + 7 * 12], F32, kind="ExternalInput")
        self.x1 = dt("x1s", [NDC, 128, NT], F32)
        self.cat = dt("cats", [NDC, 128, NT], BF16)
        self.wgu_b = dt("wgub", [depth, 2, NFC, 128, 2 * NDC * 128], BF16)
        self.wd_b = dt("wdb", [depth, 2, NDC, 128, NFC * 128], BF16)
        self.conv_pending = []
        self.conv_lane = None
        self._setup_persistent()

    def _setup_persistent(self):
        nc, P, M = self.nc, self.P, self.M
        self.ps = [nc.alloc_psum_tensor("psb%d" % i, [128, 512], F32) for i in range(8)]
        self.bankb = [Buf() for _ in range(8)]
        self.psb = [Buf(self.bankb[i]) for i in range(8)]
        self.CST = M.alloc([128, C_N, 128], F32, "cst")
        self.CSTB = M.alloc([128, 2, 128], BF16, "cstb")
        self.PRM = M.alloc([128, self.depth, PRM_N], F32, "prm")
        self.BC = M.alloc([128, self.depth, BC_N], F32, "bc")
        self.AB = M.alloc([128, self.depth, 16], F32, "ab")
        self.PC = M.alloc([128, 1 + 84], F32, "pc")
        self.EPS = M.alloc([128, 4], F32, "eps")
        self.CVAL = M.alloc([128, 64], F32, "cval")
        b = self.bconst = Buf()
        ln = P.lane("const")
        sp = P.sp
        P.dma(sp, ln, lambda: nc.sync.dma_start(out=self.CST[:], in_=self.cst.ap()), writes=[b])
        P.dma(sp, ln, lambda: nc.sync.dma_start(out=self.PRM[:], in_=self.prm.ap().rearrange("l p n -> p l n")), writes=[b])
        P.dma(sp, ln, lambda: nc.sync.dma_start(out=self.BC[:], in_=self.bc.ap().rearrange("l p n -> p l n")), writes=[b])
        P.dma(sp, ln, lambda: nc.sync.dma_start(out=self.PC[:], in_=self.percore.ap()), writes=[b])
        P.dma(sp, ln, lambda: nc.sync.dma_start(out=self.CVAL[:], in_=self.cval.ap()), writes=[b])
        b2 = Buf()
        P.op(P.dve, lambda: nc.vector.tensor_copy(out=self.CSTB[:, 0, :], in_=self.CST[:, C_ID, :]), reads=[b], writes=[b2])
        P.op(P.dve, lambda: nc.vector.tensor_copy(out=self.CSTB[:, 1, :], in_=self.CST[:, C_BLK, :]), reads=[b], writes=[b2])
        P.op(P.dve, lambda: nc.vector.memset(self.EPS[:, 0:1], 4 * LN_EPS), writes=[b2])
        P.op(P.dve, lambda: nc.vector.memset(self.EPS[:, 1:2], LN_EPS), writes=[b2])
        P.op(P.dve, lambda: nc.vector.memset(self.EPS[:, 2:3], RMS_EPS), writes=[b2])
        P.op(P.dve, lambda: nc.vector.memset(self.EPS[:, 3:4], 1.0), writes=[b2])
        for l in range(self.depth):
            P.op(P.act, lambda l=l: nc.scalar.activation(out=self.AB[:, l, :], in_=self.BC[:, l, BC_ALOG:BC_ALOG + 16], func=AF.Exp),
                 reads=[b], writes=[b2])
            P.op(P.act, lambda l=l: nc.scalar.mul(out=self.AB[:, l, :], in_=self.AB[:, l, :], mul=-1.0), reads=[b2], writes=[b2])
        P.barrier()
        self.mark0 = M.mark()
        self.l_xbt = [P.lane("xbt0"), P.lane("xbt1")]
        self._lw = [P.lane("mw%d" % i) for i in range(3)]
        self._lcs = P.lane("cats")
        self.l_tt = P.lane("tt")

    def cm(self, i):
        return self.CST[:, i, :]

    def _setup_wstream(self):
        M, P = self.M, self.P
        self.NSLOT = 4
        self.wslot = [M.alloc([128, NFC * 128], BF16, "wslot") for _ in range(self.NSLOT)]
        self.wslot_b = [Buf() for _ in range(self.NSLOT)]
        if not hasattr(self, "wslot_l"):
            self.wslot_l = [P.lane("w%d" % i) for i in range(self.NSLOT)]
        self.wissued = 0
        self.wused = 0

    def _w_issue_upto(self, n):
        nc, P = self.nc, self.P
        while self.wissued < min(n, len(self.wreq)):
            i = self.wissued
            src, ncols = self.wreq[i]
            s = i % self.NSLOT
            P.dma(P.pool, self.wslot_l[s],
                  lambda s=s, src=src, ncols=ncols: nc.gpsimd.dma_start(out=self.wslot[s][:, 0:ncols], in_=src),
                  writes=[self.wslot_b[s]])
            self.wissued += 1

    def w_next(self, src, ncols):
        if self.P.dry:
            self.wreq.append((src, ncols))
            return 0
        i = self.wused
        self._w_issue_upto(i + self.NSLOT - 1)
        self.wused += 1
        return i % self.NSLOT

    def _setup_chain(self):
        M = self.M
        mk = lambda fn: [fn() for _ in range(2)]
        self.XT = mk(lambda: M.alloc([128, NDC, TT], F32, "XT"))
        self.XB = mk(lambda: M.alloc([128, NDC, TT], BF16, "XB"))
        self.HT = mk(lambda: M.alloc([128, NFC, TT], BF16, "HT"))
        self.SG = mk(lambda: M.alloc([128, TT], F32, "SG"))
        self.SQ = mk(lambda: [M.alloc([128, TT], F32, "SQ") for _ in range(2)])
        self.STA = mk(lambda: M.alloc([128, TT], F32, "STA"))
        self.STB = mk(lambda: M.alloc([128, TT], F32, "STB"))
        self.TMP = mk(lambda: [M.alloc([128, TT], F32, "TMP") for _ in range(2)])
        self.CATB = mk(lambda: M.alloc([128, NDC, TT], BF16, "CATB"))
        self.LA1 = mk(lambda: M.alloc([128, TT], F32, "LA1"))
        self.LA2 = mk(lambda: M.alloc([128, TT], F32, "LA2"))
        self.bLA1, self.bLA2 = mk(Buf), mk(Buf)
        self.WOUT = M.alloc([128, NDC, D], BF16, "WOUT")
        self.bXT = mk(lambda: [Buf() for _ in range(NDC)])
        self.bXB = mk(lambda: [Buf() for _ in range(NDC)])
        self.bHT = mk(lambda: [Buf() for _ in range(NFC)])
        self.bSG = mk(Buf)
        self.bSQ = mk(lambda: [Buf(), Buf()])
        self.bSTA, self.bSTB = mk(Buf), mk(Buf)
        self.bTMP = mk(lambda: [Buf(), Buf()])
        self.bCATB = mk(Buf)
        self.bWOUT = Buf()
        if not hasattr(self, "l_xt"):
            P = self.P
            self.l_xt = [P.lane("xt0"), P.lane("xt1")]
            self.l_xb = [P.lane("xb0"), P.lane("xb1")]
            self.l_cat = [P.lane("catb0"), P.lane("catb1")]
            self.l_wout = P.lane("wout")
            self.l_st = [P.lane("store0"), P.lane("store1")]
        self._setup_wstream()
        self.gu_ctr = 0
        self.y_ctr = 0

    def ffn(self, l, which, cx):
        nc, P = self.nc, self.P
        XT, XB, HT, SG = self.XT[cx], self.XB[cx], self.HT[cx], self.SG[cx]
        for j in range(NFC):
            g, u = ((0, 1), (2, 3))[self.gu_ctr % 2]
            self.gu_ctr += 1
            s = self.w_next((self.wgu if (l, which) == (0, 0) else self.wgu_b).ap()[l, which, j], 2 * NDC * 128)
            W = self.wslot[s]
            for c in range(NDC):
                P.op(P.pe, lambda: nc.tensor.matmul(self.ps[g][:], W[:, c * 128:(c + 1) * 128], XB[:, c, :],
                                                    start=(c == 0), stop=(c == NDC - 1)),
                     reads=[self.wslot_b[s], self.bXB[cx][c]], writes=[self.psb[g]], inc=(c == NDC - 1))
            for c in range(NDC):
                P.op(P.pe, lambda: nc.tensor.matmul(self.ps[u][:], W[:, (NDC + c) * 128:(NDC + c + 1) * 128], XB[:, c, :],
                                                    start=(c == 0), stop=(c == NDC - 1)),
                     reads=[self.wslot_b[s], self.bXB[cx][c]], writes=[self.psb[u]], inc=(c == NDC - 1))
            P.op(P.act, lambda: nc.scalar.activation(out=SG[:], in_=self.ps[g][:], func=AF.Silu),
                 reads=[self.psb[g]], writes=[self.bSG[cx]])
            P.op(P.dve, lambda: nc.vector.tensor_tensor(out=HT[:, j, :], in0=SG[:], in1=self.ps[u][:], op=ALU.mult),
                 reads=[self.bSG[cx], self.psb[u]], writes=[self.bHT[cx][j]])
            yield
        for i in range(NDC):
            y = 4 + (self.y_ctr % 2)
            self.y_ctr += 1
            s = self.w_next((self.wd if (l, which) == (0, 0) else self.wd_b).ap()[l, which, i], NFC * 128)
            W = self.wslot[s]
            for j in range(NFC):
                P.op(P.pe, lambda: nc.tensor.matmul(self.ps[y][:], W[:, j * 128:(j + 1) * 128], HT[:, j, :],
                                                    start=(j == 0), stop=(j == NFC - 1)),
                     reads=[self.wslot_b[s], self.bHT[cx][j]], writes=[self.psb[y]], inc=(j == NFC - 1))
            P.op(P.dve, lambda: nc.vector.scalar_tensor_tensor(out=XT[:, i, :], in0=XT[:, i, :], scalar=2.0 * ALPHA,
                                                               in1=self.ps[y][:], op0=ALU.mult, op1=ALU.add),
                 reads=[self.bXT[cx][i], self.psb[y]], writes=[self.bXT[cx][i]])
            yield

    def layernorm(self, l, which, eps_col, cx):
        nc, P = self.nc, self.P
        XT, XB, SQ, STA, STB, TMP = self.XT[cx], self.XB[cx], self.SQ[cx], self.STA[cx], self.STB[cx], self.TMP[cx]
        bXT, bXB, bSQ, bSTA, bSTB, bTMP = self.bXT[cx], self.bXB[cx], self.bSQ[cx], self.bSTA[cx], self.bSTB[cx], self.bTMP[cx]
        S1, S2 = 6, 7
        A1, A2, bA1, bA2 = self.LA1[cx], self.LA2[cx], self.bLA1[cx], self.bLA2[cx]
        for c in range(NDC):
            k = c % 2
            P.op(P.act, lambda: nc.scalar.activation(out=SQ[k][:], in_=XT[:, c, :], func=AF.Square),
                 reads=[bXT[c]], writes=[bSQ[k]])
            if c == 1:
                P.op(P.dve, lambda: nc.vector.tensor_tensor(out=A1[:], in0=XT[:, 0, :], in1=XT[:, 1, :], op=ALU.add),
                     reads=[bXT[0], bXT[1]], writes=[bA1])
                P.op(P.dve, lambda: nc.vector.tensor_tensor(out=A2[:], in0=SQ[0][:], in1=SQ[1][:], op=ALU.add),
                     reads=[bSQ[0], bSQ[1]], writes=[bA2])
            elif c > 1:
                P.op(P.dve, lambda: nc.vector.tensor_tensor(out=A1[:], in0=A1[:], in1=XT[:, c, :], op=ALU.add),
                     reads=[bA1, bXT[c]], writes=[bA1])
                P.op(P.dve, lambda: nc.vector.tensor_tensor(out=A2[:], in0=A2[:], in1=SQ[k][:], op=ALU.add),
                     reads=[bA2, bSQ[k]], writes=[bA2])
            if c % 4 == 3:
                yield
        P.op(P.pe, lambda: nc.tensor.matmul(self.ps[S1][:], self.cm(C_ONESD), A1[:], start=True, stop=True),
             reads=[bA1, self.bconst], writes=[self.psb[S1]])
        P.op(P.pe, lambda: nc.tensor.matmul(self.ps[S2][:], self.cm(C_ONESD), A2[:], start=True, stop=True),
             reads=[bA2, self.bconst], writes=[self.psb[S2]])
        P.op(P.act, lambda: nc.scalar.copy(out=STA[:], in_=self.ps[S1][:]), reads=[self.psb[S1]], writes=[bSTA])
        P.op(P.dve, lambda: nc.vector.tensor_tensor(out=STB[:], in0=STA[:], in1=STA[:], op=ALU.mult),
             reads=[bSTA], writes=[bSTB])
        P.op(P.dve, lambda: nc.vector.tensor_tensor(out=STB[:], in0=self.ps[S2][:], in1=STB[:], op=ALU.subtract),
             reads=[self.psb[S2], bSTB], writes=[bSTB])
        yield
        P.op(P.act, lambda: nc.scalar.activation(out=STB[:], in_=STB[:], func=AF.Ln, bias=self.EPS[:, eps_col:eps_col + 1]),
             reads=[bSTB], writes=[bSTB])
        P.op(P.act, lambda: nc.scalar.activation(out=STB[:], in_=STB[:], func=AF.Exp, scale=-0.5),
             reads=[bSTB], writes=[bSTB])
        P.op(P.dve, lambda: nc.vector.scalar_tensor_tensor(out=STA[:], in0=STA[:], scalar=-1.0, in1=STB[:],
                                                           op0=ALU.mult, op1=ALU.mult),
             reads=[bSTA, bSTB], writes=[bSTA])
        yield
        gcol = PRM_LN + which * 16
        for c in range(NDC):
            k = c % 2
            P.op(P.dve, lambda: nc.vector.tensor_tensor(out=TMP[k][:], in0=XT[:, c, :], in1=STB[:], op=ALU.mult),
                 reads=[bXT[c], bSTB], writes=[bTMP[k]])
            P.op(P.dve, lambda: nc.vector.tensor_tensor(out=TMP[k][:], in0=TMP[k][:], in1=STA[:], op=ALU.add),
                 reads=[bTMP[k], bSTA], writes=[bTMP[k]])
            P.op(P.act, lambda: nc.scalar.activation(out=XT[:, c, :], in_=TMP[k][:], func=AF.Identity,
                                                     scale=self.PRM[:, l, gcol + c:gcol + c + 1],
                                                     bias=self.PRM[:, l, gcol + 8 + c:gcol + 9 + c]),
                 reads=[bTMP[k], self.bconst], writes=[bXT[c]])
            P.op(P.act, lambda: nc.scalar.activation(out=XB[:, c, :], in_=TMP[k][:], func=AF.Identity,
                                                     scale=self.PRM[:, l, gcol + c:gcol + c + 1],
                                                     bias=self.PRM[:, l, gcol + 8 + c:gcol + 9 + c]),
                 reads=[bTMP[k], self.bconst], writes=[bXB[c]])
            if c % 2 == 1:
                yield

    def load_x_tile(self, src, t0, with_bf16, cx):
        nc, P = self.nc, self.P
        sv = src.ap()[:, :, t0:t0 + TT].rearrange("c p t -> p c t")
        P.dma(P.sp, self.l_xt[cx], lambda: nc.sync.dma_start(out=self.XT[cx][:], in_=sv), writes=self.bXT[cx])
        if with_bf16:
            P.dma(P.pool, self.l_xb[cx], lambda: nc.gpsimd.dma_start(out=self.XB[cx][:], in_=sv), writes=self.bXB[cx])

    def load_cat_tile(self, t0, cx):
        nc, P = self.nc, self.P
        cv = self.cat.ap()[:, :, t0:t0 + TT].rearrange("c p t -> p c t")
        P.dma(P.sp, self.l_cat[cx], lambda: nc.sync.dma_start(out=self.CATB[cx][:], in_=cv), writes=[self.bCATB[cx]])

    def store_x_tile(self, dst, t0, cx):
        nc, P = self.nc, self.P
        dv = dst.ap()[:, :, t0:t0 + TT].rearrange("c p t -> p c t")
        P.dma(P.sp, self.l_st[cx], lambda: nc.sync.dma_start(out=dv, in_=self.XT[cx][:]), reads=self.bXT[cx])

    def outproj(self, l, t0, cx):
        nc, P = self.nc, self.P
        XT, CATB = self.XT[cx], self.CATB[cx]
        for i in range(NDC):
            y = 4 + (self.y_ctr % 2)
            self.y_ctr += 1
            for c in range(NDC):
                P.op(P.pe, lambda: nc.tensor.matmul(self.ps[y][:], self.WOUT[:, c, i * 128:(i + 1) * 128], CATB[:, c, :],
                                                    start=(c == 0), stop=(c == NDC - 1)),
                     reads=[self.bWOUT, self.bCATB[cx]], writes=[self.psb[y]], inc=(c == NDC - 1))
            P.op(P.dve, lambda: nc.vector.scalar_tensor_tensor(out=XT[:, i, :], in0=XT[:, i, :], scalar=ALPHA,
                                                               in1=self.ps[y][:], op0=ALU.mult, op1=ALU.add),
                 reads=[self.bXT[cx][i], self.psb[y]], writes=[self.bXT[cx][i]])
            if i % 2 == 1:
                yield

    def _run_tiles(self, tile_gen, ntile):
        def drive():
            gens = {}
            nxt = 0
            steps = 0
            lead = None
            while nxt < ntile or gens:
                if nxt < ntile and len(gens) < 2 and (not gens or lead is None or steps >= lead):
                    cx = [c for c in (0, 1) if c not in gens][0]
                    gens[cx] = tile_gen(nxt, cx)
                    nxt += 1
                for cx in list(gens):
                    try:
                        next(gens[cx])
                    except StopIteration:
                        del gens[cx]
                steps += 1
                if lead is None and nxt == 1 and 0 not in gens:
                    lead = 0
                if lead is None and nxt == 1:
                    lead = self._half_steps
        P = self.P
        self.wreq = []
        P.dry = True
        drive()
        P.dry = False
        drive()

    def stage_A(self):
        m = self.M.mark()
        self._setup_chain()
        ntile = self.cfg.ntok // TT
        dst = self.x1 if "B" in self.stages or "C" in self.stages else self.yT
        self._half_steps = 20

        def tile(t, cx):
            self.load_x_tile(self.xT, t * TT, True, cx)
            for _ in range(self.NPRE):
                yield
            yield from self.ffn(0, 0, cx)
            yield from self.layernorm(0, 0, 0, cx)
            self.store_x_tile(dst, t * TT, cx)
        self._run_tiles(tile, ntile)
        self.P.barrier()
        self.M.reset(m)

    def stage_C(self, l):
        nc, P = self.nc, self.P
        m = self.M.mark()
        self._setup_chain()
        last = (l == self.depth - 1)
        ntile = self.cfg.ntok // TT
        P.dma(P.pool, self.l_wout, lambda: nc.gpsimd.dma_start(out=self.WOUT[:], in_=self.wout.ap()[l]), writes=[self.bWOUT])
        self._half_steps = 28

        def tile(t, cx):
            self.load_x_tile(self.x1, t * TT, False, cx)
            self.load_cat_tile(t * TT, cx)
            for _ in range(self.NPRE):
                yield
            yield from self.outproj(l, t * TT, cx)
            yield from self.layernorm(l, 1, 1, cx)
            yield from self.ffn(l, 1, cx)
            yield from self.layernorm(l, 2, 0, cx)
            if not last:
                yield from self.ffn(l + 1, 0, cx)
                yield from self.layernorm(l + 1, 0, 0, cx)
                self.store_x_tile(self.x1, t * TT, cx)
            else:
                self.store_x_tile(self.yT, t * TT, cx)
        self._run_tiles(tile, ntile)
        P.barrier()
        self.M.reset(m)

    def dbg_copy_in(self):
        nc, P = self.nc, self.P
        ln = P.lane("dbgin")
        b = Buf()
        for c in range(NDC):
            P.dma(P.sp, ln, lambda: nc.sync.dma_start(out=self.x1.ap()[c], in_=self.xT.ap()[c]), writes=[b])
        P.barrier()

    def dbg_cat_out(self):
        nc, P = self.nc, self.P
        m = self.M.mark()
        self._setup_chain()
        for t in range(self.cfg.ntok // TT):
            cv = self.cat.ap()[:, :, t * TT:(t + 1) * TT].rearrange("c p t -> p c t")
            P.dma(P.sp, self.l_cat[0], lambda: nc.sync.dma_start(out=self.CATB[0][:], in_=cv), writes=[self.bCATB[0]])
            P.op(P.dve, lambda: nc.vector.tensor_copy(out=self.XT[0][:], in_=self.CATB[0][:]), reads=[self.bCATB[0]], writes=self.bXT[0])
            self.store_x_tile(self.yT, t * TT, 0)
        P.barrier()
        self.M.reset(m)

    def build(self):
        if "Tin" in self.stages:
            self.dbg_copy_in()
            for l in range(self.depth if "B2" in self.stages else 1):
                self.stage_B(l)
            self.dbg_cat_out()
            self.P.barrier()
            return self.nc
        if "A" in self.stages:
            self.stage_A()
        for l in range(self.depth):
            if "B" in self.stages:
                self.stage_B(l)
            if "C" in self.stages:
                self.stage_C(l)
        self.P.barrier()
        return self.nc

    def stage_B(self, l):
        P = self.P
        m = self.M.mark()
        if l == 0 and "C" in self.stages:
            self._conv_plan()
        self._setup_mixer(l)
        m2 = self.M.mark()
        for (s0, ns) in self.cfg.groups:
            if "att" in self.mix_parts:
                self.attention(l, s0, ns)
                P.barrier()
                self.M.reset(m2)
            if "ssd" in self.mix_parts:
                self.ssd(l, s0, ns)
                P.barrier()
                self.M.reset(m2)
        if self.conv_pending:
            self._conv_issue(len(self.conv_pending))
            P.barrier()
        self.M.reset(m)

    mix_parts = ("att", "ssd")
    NPRE = 6

    def _setup_mixer(self, l):
        nc, P, M = self.nc, self.P, self.M
        self.TT = M.alloc([128, 4, 960], F32, "TT")
        self.bTT = Buf()
        G = M.alloc([128, 960], F32, "G")
        bG = Buf()
        ET = M.alloc([128, 512], F32, "ET")
        bET = Buf()
        for hp in range(4):
            for hpar in range(2):
                src = AP(self.rpbf, ((l * 8 + 2 * hp + hpar) * 15) * 127, [[1, 64], [127, 15], [1, 64]])
                P.dma(P.sp, self.l_tt, lambda: nc.sync.dma_start(out=G[hpar * 64:(hpar + 1) * 64, :].rearrange("p (r q) -> p r q", q=64), in_=src),
                      writes=[bG])
            for (c0, n) in ((0, 512), (512, 448)):
                P.op(P.pe, lambda: nc.tensor.matmul(self.ps[7][:, 0:n], self.cm(C_J2), G[:, c0:c0 + n], start=True, stop=True),
                     reads=[bG, self.bconst], writes=[self.psb[7]])
                P.op(P.act, lambda: nc.scalar.activation(out=ET[:, 0:n], in_=self.ps[7][:, 0:n], func=AF.Exp), reads=[self.psb[7]], writes=[bET])
                P.op(P.dve, lambda: nc.vector.tensor_tensor(out=self.TT[:, hp, c0:c0 + n].rearrange("p (r q) -> p r q", q=64),
                                                            in0=ET[:, 0:n].rearrange("p (r q) -> p r q", q=64),
                                                            in1=view(self.CVAL[:, 0:1], [[0, n // 64], [1, 64]]), op=ALU.mult),
                     reads=[bET, self.bconst], writes=[self.bTT])
        P.barrier()

    def _load_win(self, l, dst, c0, n, lane, buf):
        nc, P = self.nc, self.P
        P.dma(P.pool, lane, lambda: nc.gpsimd.dma_start(out=dst[:], in_=self.win.ap()[l, :, :, c0:c0 + n]), writes=[buf])

    def _conv_plan(self):
        for (l, w) in [(l, w) for l in range(self.depth) for w in range(2) if (l, w) != (0, 0)]:
            for j in range(NFC):
                self.conv_pending.append((self.wgu_b.ap()[l, w, j], self.wgu.ap()[l, w, j]))
            for i in range(NDC):
                self.conv_pending.append((self.wd_b.ap()[l, w, i], self.wd.ap()[l, w, i]))
        self.conv_lane = self.P.lane("wconv")
        self.bconv = Buf()

    def _conv_issue(self, n=1):
        nc, P = self.nc, self.P
        for _ in range(n):
            if not self.conv_pending:
                return
            dst, src = self.conv_pending.pop(0)
            P.dma(P.pool, self.conv_lane, lambda: nc.gpsimd.dma_start(out=dst, in_=src), writes=[self.bconv])

    def _load_xbt(self, tok0, slot):
        nc, P = self.nc, self.P
        sv = self.x1.ap()[:, :, tok0:tok0 + TT].rearrange("c p t -> p c t")
        P.dma(P.pool, self.l_xbt[slot], lambda: nc.gpsimd.dma_start(out=self.XBT[slot][:], in_=sv), writes=[self.bXBT[slot]])
        self._conv_issue(1)

    def _evac(self, k, out, in_, reads, writes):
        nc, P = self.nc, self.P
        if k % 2 == 0:
            P.op(P.act, lambda: nc.scalar.copy(out=out, in_=in_), reads=reads, writes=writes)
        else:
            P.op(P.dve, lambda: nc.vector.tensor_copy(out=out, in_=in_), reads=reads, writes=writes)

    def attention(self, l, s0, ns):
        nc, P, M, cfg = self.nc, self.P, self.M, self.cfg
        T = ns * cfg.seg
        R = T // GW
        tok0 = s0 * cfg.seg
        ntile = T // TT
        bd = (ns == 1)
        if bd:
            KT = M.alloc([128, 4, R, 128], BF16, "KBD")
            VB = M.alloc([128, R, 4, 128], BF16, "VBD")
        else:
            KT = M.alloc([128, 4, T], BF16, "KT")
            VB = M.alloc([128, R, 4, 64], BF16, "VB")
        QT = [M.alloc([128, 4, TT], BF16, "QT") for _ in range(2)]
        ATT = M.alloc([128, 4, TT], F32, "ATT")
        self.XBT = [M.alloc([128, NDC, TT], BF16, "XBT") for _ in range(2)]
        WQ = M.alloc([128, NDC, 512], BF16, "WQ")
        WK = M.alloc([128, NDC, 512], BF16, "WK")
        WV = M.alloc([128, NDC, 512], BF16, "WV")
        E = [M.alloc([128, 512], F32, "E") for _ in range(2)]
        PB = [M.alloc([128, 512], BF16, "PB") for _ in range(2)]
        PM = M.alloc([128, 512], F32, "PM")
        RD = [M.alloc([128, 64], F32, "RD") for _ in range(2)]
        SQa = [M.alloc([128, TT], F32, "SQa") for _ in range(2)]
        RSa = M.alloc([128, TT], F32, "RSa")
        TMa = [M.alloc([128, TT], F32, "TMa") for _ in range(2)]
        CATA = M.alloc([128, 4, TT], BF16, "CATA")
        bKT, bVB = Buf(), Buf()
        if bd:
            for hpar in range(2):
                hs = slice(hpar * 64, (hpar + 1) * 64)
                oc = slice((1 - hpar) * 64, (2 - hpar) * 64)
                P.op(P.pool, lambda: nc.gpsimd.memset(KT[hs, :, :, oc], 0.0), writes=[bKT])
                P.op(P.dve, lambda: nc.vector.memset(VB[hs, :, :, oc], 0.0), writes=[bVB])
        bQT = [Buf(), Buf()]
        bATT = [Buf() for _ in range(4)]
        self.bXBT = [Buf(), Buf()]
        bW = [Buf(), Buf(), Buf()]
        bE, bPB, bRD, bSQa, bTMa = [Buf(), Buf()], [Buf(), Buf()], [Buf(), Buf()], [Buf(), Buf()], [Buf(), Buf()]
        bPM, bRSa, bCATA = Buf(), Buf(), Buf()
        bOD = [Buf(self.bankb[3]), Buf(self.bankb[3])]
        self._load_win(l, WK, 512, 512, self._lw[0], bW[0])
        self._load_win(l, WV, 1024, 512, self._lw[1], bW[1])
        self._load_win(l, WQ, 0, 512, self._lw[2], bW[2])
        ev = 0
        for tt in range(ntile):
            sl = tt % 2
            self._load_xbt(tok0 + tt * TT, sl)
            X = self.XBT[sl]
            for hp in range(4):
                pb = 4 + (hp % 2)
                for c in range(NDC):
                    P.op(P.pe, lambda: nc.tensor.matmul(self.ps[pb][:], WK[:, c, hp * 128:(hp + 1) * 128], X[:, c, :],
                                                        start=(c == 0), stop=(c == NDC - 1)),
                         reads=[bW[0], self.bXBT[sl]], writes=[self.psb[pb]], inc=(c == NDC - 1))
                if bd:
                    for hpar in range(2):
                        hs = slice(hpar * 64, (hpar + 1) * 64)
                        self._evac(ev, KT[hs, hp, tt * 8:(tt + 1) * 8, hpar * 64:(hpar + 1) * 64],
                                   self.ps[pb][hs, :].rearrange("p (r k) -> p r k", k=64), [self.psb[pb]], [bKT]); ev += 1
                else:
                    self._evac(ev, KT[:, hp, tt * TT:(tt + 1) * TT], self.ps[pb][:], [self.psb[pb]], [bKT]); ev += 1
            for i in range(4):
                pb = 4 + (i % 2)
                for c in range(NDC):
                    P.op(P.pe, lambda: nc.tensor.matmul(self.ps[pb][:], X[:, c, i * 128:(i + 1) * 128], WV[:, c, :],
                                                        start=(c == 0), stop=(c == NDC - 1)),
                         reads=[bW[1], self.bXBT[sl]], writes=[self.psb[pb]], inc=(c == NDC - 1))
                for a in range(2):
                    row = tt * 8 + 2 * i + a
                    for hpar in range(2):
                        src = view(self.ps[pb][a * 64:(a + 1) * 64, hpar * 64:hpar * 64 + 1], [[128, 4], [1, 64]])
                        vdst = VB[hpar * 64:(hpar + 1) * 64, row, :, hpar * 64:(hpar + 1) * 64] if bd else VB[hpar * 64:(hpar + 1) * 64, row, :, :]
                        self._evac(ev, vdst, src, [self.psb[pb]], [bVB]); ev += 1
        special = pair_special(cfg.rs) if ns == 2 else {}
        sp_idx = {r: i for i, r in enumerate(sorted(special))}
        E4 = [E[0], E[1], M.alloc([128, 512], F32, "E"), M.alloc([128, 512], F32, "E")]
        bE4 = [bE[0], bE[1], Buf(), Buf()]
        PBA = [M.alloc([128, 8, 4, 64], BF16, "PBA") for _ in range(2)]
        PBXA = [M.alloc([128, 4, 4, 64], BF16, "PBXA") for _ in range(2)]
        PB8 = [[PBA[p][:, :, hp, :] for hp in range(4)] for p in range(2)]
        PBX = [[PBXA[p][:, :, hp, :] for hp in range(4)] for p in range(2)]
        bPB8 = [[Buf() for _ in range(4)] for _ in range(2)]
        bPBX = [[Buf() for _ in range(4)] for _ in range(2)]
        RD2 = [M.alloc([128, 256], F32, "RD2") for _ in range(2)]
        bRD2 = [Buf(), Buf()]
        ATT2 = [ATT, M.alloc([128, 4, TT], F32, "ATT2")]
        bATT2 = [[Buf() for _ in range(4)] for _ in range(2)]
        bODr = [Buf(self.bankb[4]), Buf(self.bankb[5])]
        evq = [0]

        def rows_of(r):
            rows = special[r][0] if r in special else key_rows(r, R)
            return [rows[0:8]] + ([rows[8:]] if len(rows) > 8 else [])

        def q_proj(rb):
            sl = rb % 2
            X = self.XBT[sl]
            for hp in range(4):
                for c in range(NDC):
                    P.op(P.pe, lambda: nc.tensor.matmul(self.ps[6][:], WQ[:, c, hp * 128:(hp + 1) * 128], X[:, c, :],
                                                        start=(c == 0), stop=(c == NDC - 1)),
                         reads=[bW[2], self.bXBT[sl]], writes=[self.psb[6]], inc=(c == NDC - 1))
                self._evac(evq[0], QT[sl][:, hp, :], self.ps[6][:], [self.psb[6]], [bQT[sl]]); evq[0] += 1

        def stage_S(r):
            rb, rl = divmod(r, 8)
            qs = rb % 2
            par = r % 2
            chunks = rows_of(r)
            for hp in range(4):
                for ci, rc in enumerate(chunks):
                    sb = hp if ci == 0 else 7
                    n = len(rc) * 64
                    for j, kr in enumerate(rc):
                        if bd:
                            P.op(P.pe, lambda: nc.tensor.matmul(self.ps[sb][:, j * 64:(j + 1) * 64], KT[:, hp, kr, :],
                                                                QT[qs][:, hp, rl * 64:(rl + 1) * 64], start=True, stop=True),
                                 reads=[bKT, bQT[qs]], writes=[self.psb[sb]], inc=(j == len(rc) - 1))
                            continue
                        for hpar in range(2):
                            hs = slice(hpar * 64, (hpar + 1) * 64)
                            last = (j == len(rc) - 1 and hpar == 1)
                            P.op(P.pe, lambda: nc.tensor.matmul(self.ps[sb][hs, j * 64:(j + 1) * 64], KT[hs, hp, kr * 64:(kr + 1) * 64],
                                                                QT[qs][hs, hp, rl * 64:(rl + 1) * 64], start=True, stop=True),
                                 reads=[bKT, bQT[qs]], writes=[self.psb[sb]], inc=last)
                    P.op(P.act, lambda: nc.scalar.activation(out=E4[hp][:, 0:n], in_=self.ps[sb][:, 0:n], func=AF.Exp, scale=0.125),
                         reads=[self.psb[sb]], writes=[bE4[hp]])
                    m0 = rc[0] - r + 7
                    assert 0 <= m0 and m0 + len(rc) <= 15, (r, rc)
                    dst, dbuf = (PB8[par][hp], bPB8[par][hp]) if ci == 0 else (PBX[par][hp], bPBX[par][hp])
                    if r in special:
                        P.op(P.dve, lambda: nc.vector.tensor_tensor(out=PM[:, 0:n], in0=E4[hp][:, 0:n], in1=self.TT[:, hp, m0 * 64:m0 * 64 + n],
                                                                    op=ALU.mult), reads=[bE4[hp], self.bTT], writes=[bPM])
                        col = 1 + sp_idx[r] * 12 + (0 if ci == 0 else 8)
                        P.op(P.dve, lambda: nc.vector.tensor_tensor(out=dst[:, 0:len(rc), :],
                                                                    in0=PM[:, 0:n].rearrange("p (j q) -> p j q", q=64),
                                                                    in1=view(self.PC[:, col:col + 1], [[1, len(rc)], [0, 64]]), op=ALU.mult),
                             reads=[bPM, self.bconst], writes=[dbuf])
                    else:
                        P.op(P.dve, lambda: nc.vector.tensor_tensor(out=dst[:, 0:len(rc), :], in0=E4[hp][:, 0:n].rearrange("p (j q) -> p j q", q=64),
                                                                    in1=self.TT[:, hp, m0 * 64:m0 * 64 + n].rearrange("p (j q) -> p j q", q=64),
                                                                    op=ALU.mult), reads=[bE4[hp], self.bTT], writes=[dbuf])

        def stage_V(r):
            rb, rl = divmod(r, 8)
            par = r % 2
            ab = rb % 2
            chunks = rows_of(r)
            ob = 4 + par
            tot = sum(len(rc) for rc in chunks)
            srcs = lambda hp: [(PB8[par][hp], bPB8[par][hp]), (PBX[par][hp], bPBX[par][hp])]
            for hp in range(4):
                g = 0
                for ci, rc in enumerate(chunks):
                    Pt, bP = srcs(hp)[ci]
                    for j, kr in enumerate(rc):
                        if bd:
                            P.op(P.pe, lambda: nc.tensor.matmul(self.ps[ob][:, hp * 64:(hp + 1) * 64], VB[:, kr, hp, :], Pt[:, j, :],
                                                                start=(g == 0), stop=(g == tot - 1)),
                                 reads=[bVB, bP], writes=[bODr[par]], inc=False)
                            g += 1
                            continue
                        for hpar in range(2):
                            hs = slice(hpar * 64, (hpar + 1) * 64)
                            P.op(P.pe, lambda: nc.tensor.matmul(self.ps[ob][hs, hp * 64:(hp + 1) * 64], VB[hs, kr, hp, :], Pt[hs, j, :],
                                                                start=(g == 0), stop=(g == tot - 1)),
                                 reads=[bVB, bP], writes=[bODr[par]], inc=False)
                        g += 1
            g = 0
            for ci, rc in enumerate(chunks):
                PA = (PBA, PBXA)[ci][par]
                bPs = [srcs(hp)[ci][1] for hp in range(4)]
                for j, kr in enumerate(rc):
                    P.op(P.pe, lambda: nc.tensor.matmul(self.ps[ob][:, 256:512], self.CSTB[:, 1, :], PA[:, j, :, :],
                                                        start=(g == 0), stop=(g == tot - 1)),
                         reads=bPs + [self.bconst], writes=[bODr[par]], inc=(g == tot - 1))
                    g += 1
            P.op(P.dve, lambda: nc.vector.reciprocal(out=RD2[par][:], in_=self.ps[ob][:, 256:512]), reads=[bODr[par]], writes=[bRD2[par]])
            P.op(P.dve, lambda: nc.vector.tensor_tensor(out=ATT2[ab][:, :, rl * 64:(rl + 1) * 64],
                                                        in0=self.ps[ob][:, 0:256].rearrange("p (h q) -> p h q", q=64),
                                                        in1=RD2[par][:].rearrange("p (h q) -> p h q", q=64), op=ALU.mult),
                 reads=[bODr[par], bRD2[par]], writes=bATT2[ab])

        def finish_block(rb):
            ab = rb % 2
            A = ATT2[ab]
            for c in range(4):
                kk = c % 2
                P.op(P.act, lambda: nc.scalar.activation(out=SQa[kk][:], in_=A[:, c, :], func=AF.Square), reads=[bATT2[ab][c]], writes=[bSQa[kk]])
                P.op(P.pe, lambda: nc.tensor.matmul(self.ps[6][:], self.cm(C_ONES512), SQa[kk][:], start=(c == 0), stop=(c == 3)),
                     reads=[bSQa[kk], self.bconst], writes=[self.psb[6]])
            P.op(P.act, lambda: nc.scalar.activation(out=RSa[:], in_=self.ps[6][:], func=AF.Ln, bias=self.EPS[:, 2:3]),
                 reads=[self.psb[6]], writes=[bRSa])
            P.op(P.act, lambda: nc.scalar.activation(out=RSa[:], in_=RSa[:], func=AF.Exp, scale=-0.5), reads=[bRSa], writes=[bRSa])
            for c in range(4):
                kk = c % 2
                P.op(P.dve, lambda: nc.vector.tensor_tensor(out=TMa[kk][:], in0=A[:, c, :], in1=RSa[:], op=ALU.mult),
                     reads=[bATT2[ab][c], bRSa], writes=[bTMa[kk]])
                P.op(P.act, lambda: nc.scalar.activation(out=CATA[:, c, :], in_=TMa[kk][:], func=AF.Identity,
                                                         scale=self.PRM[:, l, PRM_ATTNG + c:PRM_ATTNG + c + 1]),
                     reads=[bTMa[kk], self.bconst], writes=[bCATA])
            dv = self.cat.ap()[0:4, :, tok0 + rb * TT:tok0 + (rb + 1) * TT].rearrange("c p t -> p c t")
            P.dma(P.sp, self._lcs, lambda: nc.sync.dma_start(out=dv, in_=CATA[:]), reads=[bCATA])

        self._load_xbt(tok0, 0)
        if ntile > 1:
            self._load_xbt(tok0 + TT, 1)
        q_proj(0)
        stage_S(0)
        for r in range(R):
            rb, rl = divmod(r, 8)
            if r + 1 < R:
                if (r + 1) % 8 == 0:
                    q_proj(rb + 1)
                    if rb + 2 < ntile:
                        self._load_xbt(tok0 + (rb + 2) * TT, rb % 2)
                stage_S(r + 1)
            stage_V(r)
            if rl == 7:
                finish_block(rb)

    def ssd(self, l, s0, ns):
        nc, P, M, cfg = self.nc, self.P, self.M, self.cfg
        SEG = cfg.seg
        T = ns * SEG
        tok0 = s0 * SEG
        ntile = T // TT
        NCH = T // 128
        CPS = SEG // 128
        XST = M.alloc([128, 4, T], BF16, "XST")
        BTt = M.alloc([128, 2, T], BF16, "BTt")
        CTt = M.alloc([128, 2, T], BF16, "CTt")
        self.XBT = [M.alloc([128, NDC, TT], BF16, "XBT") for _ in range(2)]
        self.bXBT = [Buf(), Buf()]
        bXST, bBT, bCT = Buf(), Buf(), Buf()
        flag = self.PC[:, 0:1]
        mk = M.mark()
        WX = M.alloc([128, NDC, 1024], BF16, "WX")
        U = M.alloc([128, 8, SEG + 4], F32, "U")
        ACC = [M.alloc([128, TT], F32, "ACC") for _ in range(2)]
        HALO = M.alloc([128, 8, 4], F32, "HALO")
        XH = M.alloc([128, NDC, 4], BF16, "XH")
        bWX, bACC, bHALO, bXH = Buf(), [Buf(), Buf()], Buf(), Buf()
        bU = [[Buf() for _ in range(8)] for _ in range(SEG // TT)]
        self._load_win(l, WX, 2048, 1024, self._lw[0], bWX)
        ev = 0
        xl = 0
        tps = SEG // TT
        if ns == 2:
            hv = self.x1.ap()[:, :, tok0 + SEG - 2:tok0 + SEG + 2].rearrange("c p t -> p c t")
            P.dma(P.pool, self._lw[1], lambda: nc.gpsimd.dma_start(out=XH[:], in_=hv), writes=[bXH])
            for cc in range(8):
                for c in range(NDC):
                    P.op(P.pe, lambda: nc.tensor.matmul(self.ps[6][:, cc * 4:cc * 4 + 4], WX[:, c, cc * 128:(cc + 1) * 128], XH[:, c, :],
                                                        start=(c == 0), stop=(c == NDC - 1)),
                         reads=[bWX, bXH], writes=[self.psb[6]], inc=(c == NDC - 1))
            P.op(P.dve, lambda: nc.vector.tensor_scalar(out=HALO[:], in0=self.ps[6][:, 0:32].rearrange("p (c t) -> p c t", t=4),
                                                        scalar1=flag, scalar2=None, op0=ALU.mult),
                 reads=[self.psb[6], self.bconst], writes=[bHALO])
        for sg in range(ns):
            if sg == 0:
                P.op(P.dve, lambda: nc.vector.memset(U[:, :, 0:2], 0.0), writes=bU[0])
            else:
                P.op(P.dve, lambda: nc.vector.tensor_copy(out=U[:, :, 0:2], in_=HALO[:, :, 0:2]), reads=[bHALO], writes=bU[0])
            if sg == ns - 1:
                P.op(P.dve, lambda: nc.vector.memset(U[:, :, SEG + 2:SEG + 4], 0.0), writes=bU[-1])
            else:
                P.op(P.dve, lambda: nc.vector.tensor_copy(out=U[:, :, SEG + 2:SEG + 4], in_=HALO[:, :, 2:4]), reads=[bHALO], writes=bU[-1])
            def conv_tile(tl):
                tt = sg * tps + tl
                lo = tl * TT
                for cc in range(8):
                    wc = PRM_CONVW + cc * 5
                    k = cc % 2
                    ub = [bU[t][cc] for t in (tl - 1, tl, tl + 1) if 0 <= t < tps]
                    P.op(P.act, lambda: nc.scalar.activation(out=ACC[k][:], in_=U[:, cc, lo:lo + TT], func=AF.Identity,
                                                             scale=self.PRM[:, l, wc:wc + 1], bias=self.PRM[:, l, PRM_CONVB + cc:PRM_CONVB + cc + 1]),
                         reads=ub + [self.bconst], writes=[bACC[k]])
                    for kk in range(1, 5):
                        P.op(P.dve, lambda: nc.vector.scalar_tensor_tensor(out=ACC[k][:], in0=U[:, cc, lo + kk:lo + kk + TT],
                                                                           scalar=self.PRM[:, l, wc + kk:wc + kk + 1], in1=ACC[k][:],
                                                                           op0=ALU.mult, op1=ALU.add),
                             reads=ub + [bACC[k], self.bconst], writes=[bACC[k]])
                    if cc < 4:
                        dst, db = XST[:, cc, tt * TT:(tt + 1) * TT], bXST
                    elif cc < 6:
                        dst, db = BTt[:, cc - 4, tt * TT:(tt + 1) * TT], bBT
                    else:
                        dst, db = CTt[:, cc - 6, tt * TT:(tt + 1) * TT], bCT
                    P.op(P.act, lambda: nc.scalar.activation(out=dst, in_=ACC[k][:], func=AF.Silu), reads=[bACC[k]], writes=[db])

            for tl in range(tps):
                tt = sg * tps + tl
                sl = xl % 2
                xl += 1
                self._load_xbt(tok0 + tt * TT, sl)
                X = self.XBT[sl]
                for cc in range(8):
                    pb = 4 + (cc % 4)
                    for c in range(NDC):
                        P.op(P.pe, lambda: nc.tensor.matmul(self.ps[pb][:], WX[:, c, cc * 128:(cc + 1) * 128], X[:, c, :],
                                                            start=(c == 0), stop=(c == NDC - 1)),
                             reads=[bWX, self.bXBT[sl]], writes=[self.psb[pb]], inc=(c == NDC - 1))
                    self._evac(ev, U[:, cc, 2 + tl * TT:2 + (tl + 1) * TT], self.ps[pb][:], [self.psb[pb]], [bU[tl][cc]]); ev += 1
                if tl > 0:
                    conv_tile(tl - 1)
            conv_tile(tps - 1)
        P.barrier()
        M.reset(mk)
        PREVF = M.alloc([128, NCH, 512], BF16, "PREVF")
        bPREVF = [Buf() for _ in range(NCH)]
        if getattr(self, "ssd_stop", "") == "p0":
            return
        WZ = M.alloc([128, NDC, 512], BF16, "WZ")
        WDT = M.alloc([128, NDC, 16], BF16, "WDT")
        bWZ, bWDT = Buf(), Buf()
        self._load_win(l, WZ, 1536, 512, self._lw[1], bWZ)
        self._load_win(l, WDT, 3072, 16, self._lw[2], bWDT)
        f32t = lambda n, nm: M.alloc([128, n], F32, nm)
        HF, HB = f32t(512, "HF"), f32t(512, "HB")
        bHF, bHB = Buf(), Buf()
        DTA = M.alloc([128, NCH, 16], F32, "DTA")
        LAA = M.alloc([128, NCH, 16], F32, "LAA")
        bDTA = Buf()

        class Ctx:
            pass

        def mkctx():
            k = Ctx()
            k.DTB, k.EX, k.DT, k.LA, k.EXP, k.W8 = f32t(16, "DTB"), f32t(16, "EX"), f32t(16, "DT"), f32t(16, "LA"), f32t(32, "EXP"), f32t(16, "W8")
            k.XS = f32t(512, "XS")
            k.BTM = M.alloc([128, 256], BF16, "BTM")
            k.XD = [M.alloc([128, 512], BF16, "XD") for _ in range(2)]
            k.XDS = M.alloc([128, 512], BF16, "XDS")
            k.ZS = f32t(512, "ZS")
            k.CBM = [M.alloc([128, 2, 128], F32, "CBM") for _ in range(2)]
            k.RL = [f32t(512, "RL") for _ in range(4)]
            k.EE = [f32t(512, "EE") for _ in range(4)]
            k.MT = [M.alloc([128, 4, 128], BF16, "MT") for _ in range(4)]
            k.PREVB = M.alloc([128, 512], BF16, "PREVB")
            k.TMPY, k.YT, k.JUNK = f32t(512, "TMPY"), f32t(512, "YT"), f32t(512, "JUNK")
            k.SS, k.RS = f32t(1, "SS"), f32t(1, "RS")
            k.YN = M.alloc([128, 512], BF16, "YN")
            k.CATS = M.alloc([128, 4, 128], BF16, "CATS")
            for nm in ("DTB", "EX", "DT", "LA", "EXP", "W8", "XS", "BTM", "XDS", "ZS", "PREVB", "TMPY", "YT", "JUNK",
                       "SS", "RS", "YN", "CATS"):
                setattr(k, "b" + nm, Buf())
            k.bRL, k.bEE, k.bMT = [Buf() for _ in range(4)], [Buf() for _ in range(4)], [Buf() for _ in range(4)]
            k.bXD = [Buf(), Buf()]
            k.bCBM = [Buf(), Buf()]
            return k

        NCX = 2 if (M.top - M.cur) >= 2 * 44500 else 1
        ctxs = [mkctx() for _ in range(NCX)]
        p_dt, p_cu = self.ps[0][:, 0:16], self.ps[0][:, 16:64]
        p_tr = self.ps[1][:].bitcast(BF16)
        p_z, p_st, p_yo = self.ps[2], self.ps[2], self.ps[7]
        p_dd = [self.ps[3], self.ps[5]]
        p_yd = [self.ps[6], self.ps[4]]
        p_cb = self.ps[4][:, 0:256]
        p_tro = self.ps[4][:, 256:512].bitcast(BF16)
        b_dt, b_cu, b_trx, b_trb, b_z, b_st, b_cb, b_tro, b_yo = (Buf(self.bankb[i]) for i in (0, 0, 1, 1, 2, 2, 4, 4, 7))
        b_dd = [Buf(self.bankb[3]), Buf(self.bankb[5])]
        b_yd = [Buf(self.bankb[6]), Buf(self.bankb[4])]
        IDB = self.CSTB[:, 0, :]
        bc = self.bconst
        dtb_bias = self.BC[:, l, BC_DTBIAS:BC_DTBIAS + 16]
        a_b = self.AB[:, l, :]
        dsk = self.BC[:, l, BC_DSKIP:BC_DSKIP + 8]
        ng = self.BC[:, l, BC_SSMG:BC_SSMG + 512]

        def bc_hp(ap8):
            return view(ap8, [[1, 8], [0, 64]])

        def v3(ap):
            return ap.rearrange("p (h d) -> p h d", d=64)

        xstate = {"xl": 0, "X": None, "sl": 0, "tile": -1}

        def xtile(c):
            t = c // 4
            if t != xstate["tile"]:
                sl = xstate["xl"] % 2
                xstate["xl"] += 1
                self._load_xbt(tok0 + t * TT, sl)
                xstate.update(tile=t, sl=sl, X=self.XBT[sl])
            return xstate["X"], xstate["sl"]

        def dt_all():
            pall = self.ps[0]
            ball = Buf(self.bankb[0])
            for c in range(NCH):
                X, sl = xtile(c)
                tk = slice((c % 4) * 128, (c % 4) * 128 + 128)
                for c8 in range(NDC):
                    P.op(P.pe, lambda: nc.tensor.matmul(pall[:, c * 16:(c + 1) * 16], X[:, c8, tk], WDT[:, c8, :], start=(c8 == 0), stop=(c8 == NDC - 1)),
                         reads=[self.bXBT[sl], bWDT], writes=[ball], inc=(c8 == NDC - 1))
            n = NCH * 16
            la2, dt2 = LAA[:].rearrange("p c s -> p (c s)"), DTA[:].rearrange("p c s -> p (c s)")
            P.op(P.dve, lambda: nc.vector.tensor_tensor(out=LAA[:], in0=pall[:, 0:n].rearrange("p (c s) -> p c s", s=16),
                                                        in1=view(dtb_bias, [[0, NCH], [1, 16]]), op=ALU.add), reads=[ball, bc], writes=[bDTA])
            P.op(P.act, lambda: nc.scalar.activation(out=dt2, in_=la2, func=AF.Exp), reads=[bDTA], writes=[bDTA])
            P.op(P.act, lambda: nc.scalar.activation(out=dt2, in_=dt2, func=AF.Ln, bias=self.EPS[:, 3:4]), reads=[bDTA, bc], writes=[bDTA])
            P.op(P.dve, lambda: nc.vector.tensor_tensor(out=LAA[:], in0=DTA[:], in1=view(a_b, [[0, NCH], [1, 16]]), op=ALU.mult),
                 reads=[bDTA, bc], writes=[bDTA])

        def prep(k, c, X, sl):
            k.DT, k.LA = DTA[:, c, :], LAA[:, c, :]
            k.bDT = k.bLA = bDTA
            cs = slice(c * 128, (c + 1) * 128)
            for cc in range(4):
                P.op(P.pe, lambda: nc.tensor.transpose(p_tr[:, cc * 128:(cc + 1) * 128], XST[:, cc, cs], IDB),
                     reads=[bXST, bc], writes=[b_trx], inc=(cc == 3))
            for g in range(2):
                P.op(P.pe, lambda: nc.tensor.transpose(p_tr[:, 512 + g * 128:512 + (g + 1) * 128], BTt[:, g, cs], IDB),
                     reads=[bBT, bc], writes=[b_trb], inc=(g == 1))
            P.op(P.act, lambda: nc.scalar.copy(out=k.XS[:], in_=p_tr[:, 0:512]), reads=[b_trx], writes=[k.bXS])
            P.op(P.dve, lambda: nc.vector.tensor_copy(out=k.BTM[:], in_=p_tr[:, 512:768]), reads=[b_trb], writes=[k.bBTM])

        def states(k, H, bH, cdcol):
            for g in range(2):
                P.op(P.pe, lambda: nc.tensor.matmul(p_st[:, g * 256:(g + 1) * 256], k.BTM[:, g * 128:(g + 1) * 128], k.XDS[:, g * 256:(g + 1) * 256],
                                                    start=True, stop=True),
                     reads=[k.bBTM, k.bXDS], writes=[b_st], inc=(g == 1))
            P.op(P.dve, lambda: nc.vector.tensor_tensor(out=v3(H[:]), in0=v3(H[:]), in1=bc_hp(k.EXP[:, cdcol:cdcol + 8]), op=ALU.mult),
                 reads=[bH, k.bEXP], writes=[bH])
            P.op(P.dve, lambda: nc.vector.tensor_tensor(out=H[:], in0=H[:], in1=p_st[:], op=ALU.add), reads=[bH, b_st], writes=[bH])

        def chunk1(c, k):
            X, sl = xtile(c)
            prep(k, c, X, sl)
            yield None
            P.op(P.pe, lambda: nc.tensor.matmul(p_cu[:, 0:8], self.cm(C_MGT), k.LA[:, 0:8], start=True, stop=True), reads=[k.bLA, bc], writes=[b_cu], inc=False)
            P.op(P.pe, lambda: nc.tensor.matmul(p_cu[:, 8:16], self.cm(C_ONES), k.LA[:, 0:8], start=True, stop=True), reads=[k.bLA, bc], writes=[b_cu])
            P.op(P.act, lambda: nc.scalar.activation(out=k.EXP[:, 0:16], in_=p_cu[:, 0:16], func=AF.Exp), reads=[b_cu], writes=[k.bEXP])
            P.op(P.dve, lambda: nc.vector.tensor_tensor(out=k.W8[:, 0:8], in0=k.DT[:, 0:8], in1=k.EXP[:, 0:8], op=ALU.mult), reads=[k.bDT, k.bEXP], writes=[k.bW8])
            P.op(P.dve, lambda: nc.vector.tensor_tensor(out=v3(k.XDS[:]), in0=v3(k.XS[:]), in1=bc_hp(k.W8[:, 0:8]), op=ALU.mult),
                 reads=[k.bXS, k.bW8], writes=[k.bXDS])
            yield "need_H"
            if c % CPS == 0:
                if c == 0:
                    P.op(P.dve, lambda: nc.vector.memset(HF[:], 0.0), writes=[bHF])
                else:
                    P.op(P.dve, lambda: nc.vector.tensor_scalar(out=HF[:], in0=HF[:], scalar1=flag, scalar2=None, op0=ALU.mult),
                         reads=[bHF, bc], writes=[bHF])
            P.op(P.act, lambda: nc.scalar.copy(out=PREVF[:, c, :], in_=HF[:]), reads=[bHF], writes=[bPREVF[c]])
            states(k, HF, bHF, 8)
            yield "done_H"

        def chunk2(c, k):
            X, sl = xtile(c)
            prep(k, c, X, sl)
            yield None
            tk = slice((c % 4) * 128, (c % 4) * 128 + 128)
            cs = slice(c * 128, (c + 1) * 128)
            for c8 in range(NDC):
                P.op(P.pe, lambda: nc.tensor.matmul(p_z[:], X[:, c8, tk], WZ[:, c8, :], start=(c8 == 0), stop=(c8 == NDC - 1)),
                     reads=[self.bXBT[sl], bWZ], writes=[b_z], inc=(c8 == NDC - 1))
            P.op(P.act, lambda: nc.scalar.activation(out=k.ZS[:], in_=p_z[:], func=AF.Silu), reads=[b_z], writes=[k.bZS])
            P.op(P.pe, lambda: nc.tensor.matmul(p_cu[:, 0:8], self.cm(C_MLE), k.LA[:, 0:8], start=True, stop=True), reads=[k.bLA, bc], writes=[b_cu], inc=False)
            P.op(P.pe, lambda: nc.tensor.matmul(p_cu[:, 8:16], self.cm(C_MGE), k.LA[:, 8:16], start=True, stop=True), reads=[k.bLA, bc], writes=[b_cu], inc=False)
            P.op(P.pe, lambda: nc.tensor.matmul(p_cu[:, 16:24], self.cm(C_MLT), k.LA[:, 8:16], start=True, stop=True), reads=[k.bLA, bc], writes=[b_cu], inc=False)
            P.op(P.pe, lambda: nc.tensor.matmul(p_cu[:, 24:32], self.cm(C_ONES), k.LA[:, 8:16], start=True, stop=True), reads=[k.bLA, bc], writes=[b_cu])
            P.op(P.act, lambda: nc.scalar.activation(out=k.EXP[:, 0:32], in_=p_cu[:, 0:32], func=AF.Exp), reads=[b_cu], writes=[k.bEXP])
            for d in range(2):
                P.op(P.pool, lambda: nc.gpsimd.tensor_tensor(out=v3(k.XD[d][:]), in0=v3(k.XS[:]), in1=bc_hp(k.DT[:, d * 8:d * 8 + 8]), op=ALU.mult),
                     reads=[k.bXS, k.bDT], writes=[k.bXD[d]])
            P.op(P.dve, lambda: nc.vector.tensor_tensor(out=k.W8[:, 0:8], in0=k.DT[:, 8:16], in1=k.EXP[:, 16:24], op=ALU.mult), reads=[k.bDT, k.bEXP], writes=[k.bW8])
            P.op(P.dve, lambda: nc.vector.tensor_tensor(out=v3(k.XDS[:]), in0=v3(k.XS[:]), in1=bc_hp(k.W8[:, 0:8]), op=ALU.mult),
                 reads=[k.bXS, k.bW8], writes=[k.bXDS])
            yield "need_H"
            if c % CPS == CPS - 1:
                if c == NCH - 1:
                    P.op(P.dve, lambda: nc.vector.memset(HB[:], 0.0), writes=[bHB])
                else:
                    P.op(P.dve, lambda: nc.vector.tensor_scalar(out=HB[:], in0=HB[:], scalar1=flag, scalar2=None, op0=ALU.mult),
                         reads=[bHB, bc], writes=[bHB])
            P.op(P.act, lambda: nc.scalar.copy(out=k.PREVB[:], in_=HB[:]), reads=[bHB], writes=[k.bPREVB])
            states(k, HB, bHB, 24)
            yield "done_H"
            for g in range(2):
                P.op(P.pe, lambda: nc.tensor.matmul(p_cb[:, g * 128:(g + 1) * 128], BTt[:, g, cs], CTt[:, g, cs], start=True, stop=True),
                     reads=[bBT, bCT], writes=[b_cb], inc=(g == 1))
            for d, mi in ((0, C_MLE), (1, C_MGE)):
                P.op(P.dve, lambda: nc.vector.tensor_tensor(out=k.CBM[d][:], in0=p_cb.rearrange("p (g s) -> p g s", s=128),
                                                            in1=view(self.cm(mi)[:, 0:1], [[0, 2], [1, 128]]), op=ALU.mult),
                     reads=[b_cb, bc], writes=[k.bCBM[d]])
            P.op(P.dve, lambda: nc.vector.tensor_tensor(out=v3(k.YT[:]), in0=v3(k.XS[:]), in1=bc_hp(dsk), op=ALU.mult), reads=[k.bXS, bc], writes=[k.bYT])
            yield None
            dgs = [(d, g) for d in range(2) for g in range(2)]
            for q, (d, g) in enumerate(dgs):
                mtri = (C_MLE, C_MGE)[d]
                col = d * 8 + g * 4
                P.op(P.pool, lambda: nc.gpsimd.tensor_tensor(out=k.RL[q][:].rearrange("p (h s) -> p h s", s=128),
                                                             in0=view(self.cm(mtri)[:, 0:1], [[0, 4], [1, 128]]),
                                                             in1=view(k.LA[:, col:col + 1], [[1, 4], [0, 128]]), op=ALU.mult),
                     reads=[k.bLA, bc], writes=[k.bRL[q]])
            yield None
            for q, (d, g) in enumerate(dgs):
                mstrict = (C_MGT, C_MLT)[d]
                P.op(P.pe, lambda: nc.tensor.matmul(p_dd[q % 2][:], self.cm(mstrict), k.RL[q][:], start=True, stop=True),
                     reads=[k.bRL[q], bc], writes=[b_dd[q % 2]])
                P.op(P.act, lambda: nc.scalar.activation(out=k.EE[q][:], in_=p_dd[q % 2][:], func=AF.Exp), reads=[b_dd[q % 2]], writes=[k.bEE[q]])
                P.op(P.dve, lambda: nc.vector.tensor_tensor(out=k.MT[q][:], in0=k.EE[q][:].rearrange("p (h s) -> p h s", s=128),
                                                            in1=view(k.CBM[d][:, g, 0:1], [[0, 4], [1, 128]]), op=ALU.mult),
                     reads=[k.bEE[q], k.bCBM[d]], writes=[k.bMT[q]])
            for q, (d, g) in enumerate(dgs):
                for h in range(4):
                    hh = g * 4 + h
                    P.op(P.pe, lambda: nc.tensor.matmul(p_yd[d][:, hh * 64:(hh + 1) * 64], k.MT[q][:, h, :], k.XD[d][:, hh * 64:(hh + 1) * 64], start=True, stop=True),
                         reads=[k.bMT[q], k.bXD[d]], writes=[b_yd[d]], inc=(h == 3))
            for d in range(2):
                P.op(P.dve, lambda: nc.vector.tensor_tensor(out=k.YT[:], in0=k.YT[:], in1=p_yd[d][:], op=ALU.add), reads=[k.bYT, b_yd[d]], writes=[k.bYT])
            yield None
            for d in range(2):
                for g in range(2):
                    if d == 0:
                        P.op(P.pe, lambda: nc.tensor.matmul(p_yo[:, g * 256:(g + 1) * 256], CTt[:, g, cs], PREVF[:, c, g * 256:(g + 1) * 256], start=True, stop=True),
                             reads=[bCT, bPREVF[c]], writes=[b_yo], inc=(g == 1))
                    else:
                        P.op(P.pe, lambda: nc.tensor.matmul(p_yo[:, g * 256:(g + 1) * 256], CTt[:, g, cs], k.PREVB[:, g * 256:(g + 1) * 256], start=True, stop=True),
                             reads=[bCT, k.bPREVB], writes=[b_yo], inc=(g == 1))
                P.op(P.dve, lambda: nc.vector.tensor_tensor(out=v3(k.TMPY[:]), in0=v3(p_yo[:]), in1=bc_hp(k.EXP[:, d * 8:d * 8 + 8]), op=ALU.mult),
                     reads=[b_yo, k.bEXP], writes=[k.bTMPY])
                P.op(P.dve, lambda: nc.vector.tensor_tensor(out=k.YT[:], in0=k.YT[:], in1=k.TMPY[:], op=ALU.add), reads=[k.bYT, k.bTMPY], writes=[k.bYT])
            yield None
            P.op(P.dve, lambda: nc.vector.tensor_tensor(out=k.YT[:], in0=k.YT[:], in1=k.ZS[:], op=ALU.mult), reads=[k.bYT, k.bZS], writes=[k.bYT])
            P.op(P.act, lambda: nc.scalar.activation(out=k.JUNK[:], in_=k.YT[:], func=AF.Square, accum_out=k.SS[:]), reads=[k.bYT], writes=[k.bJUNK, k.bSS])
            P.op(P.act, lambda: nc.scalar.activation(out=k.RS[:], in_=k.SS[:], func=AF.Ln, scale=1.0 / 512.0, bias=self.EPS[:, 2:3]), reads=[k.bSS, bc], writes=[k.bRS])
            P.op(P.act, lambda: nc.scalar.activation(out=k.RS[:], in_=k.RS[:], func=AF.Exp, scale=-0.5), reads=[k.bRS], writes=[k.bRS])
            P.op(P.dve, lambda: nc.vector.scalar_tensor_tensor(out=k.YN[:], in0=k.YT[:], scalar=k.RS[:, 0:1], in1=ng, op0=ALU.mult, op1=ALU.mult),
                 reads=[k.bYT, k.bRS, bc], writes=[k.bYN])
            yield None
            for cc in range(4):
                P.op(P.pe, lambda: nc.tensor.transpose(p_tro[:, cc * 128:(cc + 1) * 128], k.YN[:, cc * 128:(cc + 1) * 128], IDB),
                     reads=[k.bYN, bc], writes=[b_tro], inc=(cc == 3))
            P.op(P.act, lambda: nc.scalar.copy(out=k.CATS[:], in_=p_tro.rearrange("p (c t) -> p c t", t=128)), reads=[b_tro], writes=[k.bCATS])
            dv = self.cat.ap()[4:8, :, tok0 + c * 128:tok0 + (c + 1) * 128].rearrange("c p t -> p c t")
            P.dma(P.sp, self._lcs, lambda: nc.sync.dma_start(out=dv, in_=k.CATS[:]), reads=[k.bCATS])

        def run_chunks(order, gen_fn, lag):
            pos = {c: i for i, c in enumerate(order)}
            active = []
            free = list(range(NCX))
            nxt = 0
            hdone = -1
            since = lag
            while nxt < len(order) or active:
                if nxt < len(order) and free and since >= lag:
                    cx = free.pop(0)
                    c = order[nxt]
                    nxt += 1
                    active.append([c, cx, gen_fn(c, ctxs[cx]), False])
                    since = 0
                for a in list(active):
                    if a[3]:
                        if hdone == pos[a[0]] - 1:
                            a[3] = False
                        else:
                            continue
                    try:
                        tag = next(a[2])
                    except StopIteration:
                        active.remove(a)
                        free.append(a[1])
                        continue
                    if tag == "need_H" and hdone != pos[a[0]] - 1:
                        a[3] = True
                    elif tag == "done_H":
                        hdone = pos[a[0]]
                since += 1

        if getattr(self, "ssd_stop", "") == "p0":
            return
        dt_all()
        xstate["tile"] = -1
        run_chunks(list(range(NCH)), chunk1, 1)
        xstate["tile"] = -1
        run_chunks(list(range(NCH - 1, -1, -1)), chunk2, 5)


def const_mats():
    c = np.zeros((128, C_N, 128), np.float32)
    i = np.arange(128)
    c[:, C_ID] = np.eye(128)
    c[:, C_MLE] = (i[:, None] <= i[None, :])
    c[:, C_MGE] = (i[:, None] >= i[None, :])
    c[:, C_MLT] = (i[:, None] < i[None, :])
    c[:, C_MGT] = (i[:, None] > i[None, :])
    c[:, C_ONES] = 1.0
    c[:, C_ONESD] = 1.0 / D
    c[:, C_ONES512] = 0.0
    j2 = np.zeros((128, 128), np.float32)
    for h in range(2):
        for k in range(64):
            j2[h * 64 + k, h * 64 + 63 - k] = 1.0
    c[:, C_J2] = j2
    blk = np.zeros((128, 128), np.float32)
    blk[:64, :64] = 1.0
    blk[64:, 64:] = 1.0
    c[:, C_BLK] = blk
    c[:, C_ONES512] = 1.0 / 512.0
    return c


def col_valid():
    qc = np.arange(GW)
    cstart = np.clip(qc - 8, 0, GW - 16)
    kc = np.arange(GW)
    v = (kc[:, None] >= cstart[None, :]) & (kc[:, None] < cstart[None, :] + 16)
    return np.concatenate([v, v], 0).astype(np.float32)


def key_rows(r, rows):
    rs = int(np.clip(r - 4, 0, rows - 8))
    return list(range(rs, rs + 8))


def pair_special(rs_):
    out = {}
    for r in range(2 * rs_):
        kj = key_rows(r, 2 * rs_)
        seg = r // rs_
        ku = [seg * rs_ + k for k in key_rows(r - seg * rs_, rs_)]
        if kj != ku:
            lo, hi = min(kj[0], ku[0]), max(kj[-1], ku[-1])
            out[r] = (list(range(lo, hi + 1)), kj, ku)
    return out


def prep_weights(inp, depth=DEPTH):
    f = lambda a: np.ascontiguousarray(a, dtype=np.float32)
    wgu = np.zeros((depth, 2, NFC, 128, 2, NDC, 128), np.float32)
    wd = np.zeros((depth, 2, NDC, 128, NFC, 128), np.float32)
    for l in range(depth):
        for w, pre in enumerate(("ffn1", "ffn2")):
            g = np.asarray(inp[pre + "_w_gate"][l]).reshape(NDC, 128, NFC, 128)
            u = np.asarray(inp[pre + "_w_up"][l]).reshape(NDC, 128, NFC, 128)
            wgu[l, w, :, :, 0] = g.transpose(2, 1, 0, 3)
            wgu[l, w, :, :, 1] = u.transpose(2, 1, 0, 3)
            dn = np.asarray(inp[pre + "_w_down"][l]).reshape(NFC, 128, NDC, 128)
            wd[l, w] = dn.transpose(2, 1, 0, 3)
    win = np.stack([np.asarray(inp["w_in"][l]).reshape(NDC, 128, NIN).transpose(1, 0, 2) for l in range(depth)])
    wout = np.stack([np.asarray(inp["w_out"][l]).reshape(NDC, 128, D).transpose(1, 0, 2) for l in range(depth)])
    prm = np.zeros((depth, 128, PRM_N), np.float32)
    bc = np.zeros((depth, 128, BC_N), np.float32)
    rp = np.zeros((depth, 8, 15, 127), np.float32)
    for l in range(depth):
        for k, nm in enumerate(("ln1", "ln2", "ln3")):
            prm[l, :, PRM_LN + k * 16:PRM_LN + k * 16 + 8] = np.asarray(inp[nm + "_g"][l]).reshape(NDC, 128).T
            prm[l, :, PRM_LN + k * 16 + 8:PRM_LN + k * 16 + 16] = np.asarray(inp[nm + "_b"][l]).reshape(NDC, 128).T
        cw = np.asarray(inp["conv_w"][l]).reshape(5, 8, 128)
        prm[l, :, PRM_CONVW:PRM_CONVW + 40] = cw.transpose(2, 1, 0).reshape(128, 40)
        prm[l, :, PRM_CONVB:PRM_CONVB + 8] = np.asarray(inp["conv_b"][l]).reshape(8, 128).T
        prm[l, :, PRM_ATTNG:PRM_ATTNG + 4] = np.asarray(inp["attn_norm_g"][l]).reshape(4, 128).T
        bc[l, :, BC_DTBIAS:BC_DTBIAS + 16] = np.asarray(inp["dt_bias"][l]).reshape(16)[None]
        bc[l, :, BC_ALOG:BC_ALOG + 16] = np.asarray(inp["a_log"][l]).reshape(16)[None]
        bc[l, :, BC_DSKIP:BC_DSKIP + 8] = np.asarray(inp["d_skip"][l])[None]
        bc[l, :, BC_SSMG:BC_SSMG + 512] = np.asarray(inp["ssm_norm_g"][l])[None]
        pad = np.zeros((8, 15, 127), np.float32)
        pad[:, :, 48:79] = np.asarray(inp["rpb"][l])
        rp[l] = pad[:, :, ::-1]
    return dict(wgu=f(wgu.reshape(depth, 2, NFC, 128, 2 * NDC * 128)), wd=f(wd.reshape(depth, 2, NDC, 128, NFC * 128)),
                win=f(win), wout=f(wout), prm=prm, bc=bc, rpbf=f(rp), cst=const_mats(), cval=col_valid())


def percore_arr(cfg, joined):
    a = np.zeros((128, 85), np.float32)
    a[:, 0] = 1.0 if joined else 0.0
    sp = pair_special(cfg.rs)
    for i, r in enumerate(sorted(sp)):
        union, kj, ku = sp[r]
        use = kj if joined else ku
        for j, kr in enumerate(union):
            a[:, 1 + i * 12 + j] = 1.0 if kr in use else 0.0
    return a


def to_fm(x):
    return np.ascontiguousarray(x.T.reshape(NDC, 128, x.shape[0]))


def from_fm(y):
    return np.ascontiguousarray(y.reshape(D, y.shape[2]).T)


_CACHE = {}


def kernel(**inputs):
    xp = np.asarray(inputs["x_prompt"], np.float32)
    xs = np.asarray(inputs["x_sample"], np.float32)
    cfg = Cfg(2048, 5)
    if "nc" not in _CACHE:
        _CACHE["nc"] = Builder(cfg).build()
    nc = _CACHE["nc"]
    w = prep_weights(inputs)
    in_maps = []
    assign = []
    for c in range(8):
        if c < 4:
            seqs = [("s", c)] + [("p", 3 * c + i) for i in range(3)]
        else:
            seqs = [("p", 12 + 5 * (c - 4) + i) for i in range(5)]
        assign.append(seqs)
        xc = np.concatenate([xs[i] if k == "s" else xp[i] for k, i in seqs], 0)
        m = dict(w)
        m["xT"] = to_fm(xc)
        m["percore"] = percore_arr(cfg, c < 4)
        in_maps.append(m)
    res = run_bass_kernel_spmd(nc, in_maps, core_ids=list(range(8)))
    yp = np.zeros_like(xp)
    ys = np.zeros_like(xs)
    for c in range(8):
        y = from_fm(np.asarray(res.results[c]["yT"]))
        o = 0
        for k, i in assign[c]:
            n = 4096 if k == "s" else 2048
            if k == "s":
                ys[i] = y[o:o + n]
            else:
                yp[i] = y[o:o + n]
            o += n
    return (yp, ys)
```
